# Optimizing a Trainium2 kernel written in Bass

```python
import math
import jax, jax.numpy as jnp
from jax import lax
import numpy as np

D_MODEL = 1024
BATCH = 8
SEQ = 4096
DEPTH = 4

N_MIXERS = 4
HEAD_DIM = 64
Q_BLOCK = 128
ROPE_THETA = 500000.0
ROT_DIM = HEAD_DIM // 4
NORM_EPS = 1e-6
D_FF = 128 * ((8 * D_MODEL // 3 + 127) // 128)
PLE_DIM = 256
FOX_HEADS = D_MODEL // HEAD_DIM
DIL_HEADS = D_MODEL // HEAD_DIM
DIL_CONFIGS = ((128, 1), (512, 4), (2048, 16))
DIL_GROUPS = len(DIL_CONFIGS)
DIL_BAND = 128
DIFF_HEADS = D_MODEL // (2 * HEAD_DIM)
DIFF_SUBLN_EPS = 1e-5
SGU_HALF = 2 * D_MODEL
SGU_GROUPS = 8
SGU_GROUP_DIM = SGU_HALF // SGU_GROUPS
SGU_CHUNK = 128
N_FOX = (DEPTH + 3) // 4
N_DIL = (DEPTH + 2) // 4
N_DIFF = (DEPTH + 1) // 4
N_SGU = DEPTH // 4

kernel_name = "hybrid_interleaved_fox_dilated_diff_sgu"


def rms_norm(x, g, eps=NORM_EPS):
    xf = x.astype(jnp.float32)
    y = xf * lax.rsqrt(jnp.mean(xf * xf, axis=-1, keepdims=True) + eps)
    return (y * g.astype(jnp.float32)).astype(x.dtype)


def swiglu(h, w_in, w_out):
    gate, up = jnp.split(h @ w_in, 2, axis=-1)
    return (jax.nn.silu(gate) * up) @ w_out


def partial_rope(t, pos):
    half = ROT_DIM // 2
    inv_freq = ROPE_THETA ** (-jnp.arange(0, ROT_DIM, 2, dtype=jnp.float32) / ROT_DIM)
    ang = pos.astype(jnp.float32)[:, None] * inv_freq[None, :]
    bshape = (pos.shape[0],) + (1,) * (t.ndim - 3) + (half,)
    cos = jnp.cos(ang).reshape(bshape)
    sin = jnp.sin(ang).reshape(bshape)
    tf = t[..., :ROT_DIM].astype(jnp.float32)
    x1, x2 = tf[..., :half], tf[..., half:]
    rot = jnp.concatenate([x1 * cos - x2 * sin, x2 * cos + x1 * sin], axis=-1)
    return jnp.concatenate([rot.astype(t.dtype), t[..., ROT_DIM:]], axis=-1)


def to_blocks(t):
    b, s = t.shape[:2]
    return jnp.swapaxes(t.reshape((b, s // Q_BLOCK, Q_BLOCK) + t.shape[2:]), 0, 1)


def from_blocks(t):
    nb, b = t.shape[:2]
    return jnp.swapaxes(t, 0, 1).reshape((b, nb * Q_BLOCK) + t.shape[3:])


def causal_probs(q_blk, k, start, bias=None):
    s = jnp.einsum('bqhd,bkhd->bhqk', q_blk, k, preferred_element_type=jnp.float32)
    s = s * (q_blk.shape[-1] ** -0.5)
    if bias is not None:
        s = s + bias
    qpos = start + jnp.arange(q_blk.shape[1])
    kpos = jnp.arange(k.shape[1])
    s = jnp.where(kpos[None, :] <= qpos[:, None], s, -jnp.inf)
    return jax.nn.softmax(s, axis=-1)


def fox_attention(h, w_in, b_f, w_out):
    b, s, _ = h.shape
    proj = h @ w_in
    q = proj[..., :D_MODEL].reshape(b, s, FOX_HEADS, HEAD_DIM)
    k = proj[..., D_MODEL:2 * D_MODEL].reshape(b, s, FOX_HEADS, HEAD_DIM)
    v = proj[..., 2 * D_MODEL:3 * D_MODEL].reshape(b, s, FOX_HEADS, HEAD_DIM)
    log_f = jax.nn.log_sigmoid((proj[..., 3 * D_MODEL:] + b_f).astype(jnp.float32))
    c = jnp.cumsum(log_f, axis=1)
    c_k = jnp.transpose(c, (0, 2, 1))

    def block(args):
        q_blk, c_blk, start = args
        bias = jnp.transpose(c_blk, (0, 2, 1))[..., None] - c_k[:, :, None, :]
        probs = causal_probs(q_blk, k, start, bias)
        return jnp.einsum('bhqk,bkhd->bqhd', probs.astype(v.dtype), v)

    starts = jnp.arange(s // Q_BLOCK, dtype=jnp.int32) * Q_BLOCK
    o = from_blocks(lax.map(block, (to_blocks(q), to_blocks(c), starts)))
    return o.reshape(b, s, D_MODEL) @ w_out


def dilated_group(q, k, v, dilation, n_steps):
    b, s, nh, dh = q.shape
    span = dilation * DIL_BAND
    s_pad = -(-s // span) * span
    n_sub = s_pad // dilation
    nb = n_sub // DIL_BAND

    def to_res(t):
        t = jnp.pad(t, ((0, 0), (0, s_pad - s), (0, 0), (0, 0)))
        t = jnp.transpose(t.reshape(b, n_sub, dilation, nh, dh), (0, 2, 1, 3, 4))
        return t.reshape(b * dilation, nb, DIL_BAND, nh, dh)

    def with_prev(t):
        prev = jnp.pad(t, ((0, 0), (1, 0), (0, 0), (0, 0), (0, 0)))[:, :-1]
        return jnp.concatenate([prev, t], axis=2)

    qr = to_res(q)
    kk = with_prev(to_res(k))
    vv = with_prev(to_res(v))
    sc = jnp.einsum('xnqhd,xnkhd->xnhqk', qr, kk, preferred_element_type=jnp.float32) * (dh ** -0.5)
    q_idx = jnp.arange(DIL_BAND)
    k_idx = jnp.arange(2 * DIL_BAND) - DIL_BAND
    steps = q_idx[:, None] - k_idx[None, :]
    in_band = (steps >= 0) & (steps <= n_steps)
    has_prev = (jnp.arange(nb) > 0)[:, None, None] | (k_idx >= 0)[None, None, :]
    mask = in_band[None] & has_prev
    sc = jnp.where(mask[None, :, None], sc, -jnp.inf)
    lse = jax.nn.logsumexp(sc, axis=-1)
    probs = jnp.exp(sc - lse[..., None])
    o = jnp.einsum('xnhqk,xnkhd->xnqhd', probs.astype(v.dtype), vv)
    o = jnp.transpose(o.reshape(b, dilation, n_sub, nh, dh), (0, 2, 1, 3, 4)).reshape(b, s_pad, nh, dh)[:, :s]
    lse = jnp.transpose(lse, (0, 1, 3, 2)).reshape(b, dilation, n_sub, nh)
    lse = jnp.transpose(lse, (0, 2, 1, 3)).reshape(b, s_pad, nh)[:, :s]
    return o, lse


def dilated_attention(h, w_in, w_out, pos):
    b, s, _ = h.shape
    proj = (h @ w_in).reshape(b, s, DIL_GROUPS + 2, DIL_HEADS, HEAD_DIM)
    q = partial_rope(proj[:, :, :DIL_GROUPS], pos)
    k = partial_rope(proj[:, :, DIL_GROUPS], pos)
    v = proj[:, :, DIL_GROUPS + 1]
    outs, lses = [], []
    for g, (window, dilation) in enumerate(DIL_CONFIGS):
        o_g, l_g = dilated_group(q[:, :, g], k, v, dilation, window // dilation)
        outs.append(o_g)
        lses.append(l_g)
    w = jax.nn.softmax(jnp.stack(lses), axis=0)
    o = jnp.einsum('gbsh,gbshd->bshd', w.astype(v.dtype), jnp.stack(outs))
    return o.reshape(b, s, D_MODEL) @ w_out


def diff_attention(h, w_in, lam_params, subln_g, w_out, pos, layer_idx):
    b, s, _ = h.shape
    proj = h @ w_in
    q = partial_rope(proj[..., :D_MODEL].reshape(b, s, DIFF_HEADS, 2, HEAD_DIM), pos)
    k = partial_rope(proj[..., D_MODEL:2 * D_MODEL].reshape(b, s, DIFF_HEADS, 2, HEAD_DIM), pos)
    v = proj[..., 2 * D_MODEL:].reshape(b, s, DIFF_HEADS, 2 * HEAD_DIM)
    lam_init = 0.8 - 0.6 * math.exp(-0.3 * layer_idx)
    lp = lam_params.astype(jnp.float32)
    lam = jnp.exp(jnp.sum(lp[0] * lp[1])) - jnp.exp(jnp.sum(lp[2] * lp[3])) + lam_init
    k1, k2 = k[:, :, :, 0], k[:, :, :, 1]

    def block(args):
        q_blk, start = args
        p1 = causal_probs(q_blk[:, :, :, 0], k1, start)
        p2 = causal_probs(q_blk[:, :, :, 1], k2, start)
        return jnp.einsum('bhqk,bkhd->bqhd', (p1 - lam * p2).astype(v.dtype), v)

    starts = jnp.arange(s // Q_BLOCK, dtype=jnp.int32) * Q_BLOCK
    o = from_blocks(lax.map(block, (to_blocks(q), starts)))
    o = rms_norm(o, subln_g, DIFF_SUBLN_EPS) * (1.0 - lam_init)
    return o.reshape(b, s, D_MODEL) @ w_out


def chunked_sgu(h, w_in, norm_v, w_s, b_s, w_out):
    b, s, _ = h.shape
    u, v = jnp.split(jax.nn.gelu(h @ w_in, approximate=False), 2, axis=-1)
    v = rms_norm(v, norm_v).reshape(b, s // SGU_CHUNK, SGU_CHUNK, SGU_GROUPS, SGU_GROUP_DIM)
    causal = jnp.tril(jnp.ones((SGU_CHUNK, SGU_CHUNK), dtype=bool))
    w = jnp.where(causal[None], w_s, 0)
    mixed = jnp.einsum('gts,bnsgc->bntgc', w.astype(v.dtype), v) + jnp.transpose(b_s)[None, None, :, :, None]
    return (u * mixed.reshape(b, s, SGU_HALF)) @ w_out


def setup_inputs(seed: int = 0) -> dict:
    key = jax.random.key(seed)
    ks = iter(jax.random.split(key, 40))
    D = D_MODEL

    def nrm(shape, scale):
        return scale * jax.random.normal(next(ks), shape, jnp.float32)

    def gain(shape):
        return 1.0 + nrm(shape, 0.05)

    return {
        "x": nrm((BATCH, SEQ, D), 1.0),
        "p": nrm((DEPTH, BATCH, SEQ, PLE_DIM), 1.0),
        "norm_ffn1": gain((DEPTH, D)),
        "w_ffn1_in": nrm((DEPTH, D, 2 * D_FF), D ** -0.5),
        "w_ffn1_out": nrm((DEPTH, D_FF, D), D_FF ** -0.5),
        "norm_mix": gain((DEPTH, D)),
        "norm_ffn2": gain((DEPTH, D)),
        "w_ffn2_in": nrm((DEPTH, D, 2 * D_FF), D ** -0.5),
        "w_ffn2_out": nrm((DEPTH, D_FF, D), D_FF ** -0.5),
        "norm_ple": gain((DEPTH, D)),
        "w_ple_gate": nrm((DEPTH, D, D), D ** -0.5),
        "b_ple_gate": nrm((DEPTH, D), 0.02),
        "w_ple_proj": nrm((DEPTH, PLE_DIM, D), PLE_DIM ** -0.5),
        "fox_w_in": nrm((N_FOX, D, 3 * D + FOX_HEADS), D ** -0.5),
        "fox_b_f": jax.random.uniform(next(ks), (N_FOX, FOX_HEADS), jnp.float32, 1.0, 5.0),
        "fox_w_out": nrm((N_FOX, D, D), D ** -0.5),
        "dil_w_in": nrm((N_DIL, D, (DIL_GROUPS + 2) * DIL_HEADS * HEAD_DIM), D ** -0.5),
        "dil_w_out": nrm((N_DIL, DIL_HEADS * HEAD_DIM, D), D ** -0.5),
        "diff_w_in": nrm((N_DIFF, D, 3 * D), D ** -0.5),
        "diff_lambda": nrm((N_DIFF, 4, HEAD_DIM), 0.1),
        "diff_subln": gain((N_DIFF, 2 * HEAD_DIM)),
        "diff_w_out": nrm((N_DIFF, D, D), D ** -0.5),
        "sgu_w_in": nrm((N_SGU, D, 2 * SGU_HALF), D ** -0.5),
        "sgu_norm_v": gain((N_SGU, SGU_HALF)),
        "sgu_w_s": nrm((N_SGU, SGU_GROUPS, SGU_CHUNK, SGU_CHUNK), SGU_CHUNK ** -0.5),
        "sgu_b_s": 1.0 + nrm((N_SGU, SGU_GROUPS, SGU_CHUNK), 0.1),
        "sgu_w_out": nrm((N_SGU, SGU_HALF, D), SGU_HALF ** -0.5),
        "norm_final": gain((D,)),
    }


def reference(x, p, norm_ffn1, w_ffn1_in, w_ffn1_out, norm_mix, norm_ffn2, w_ffn2_in, w_ffn2_out,
              norm_ple, w_ple_gate, b_ple_gate, w_ple_proj, fox_w_in, fox_b_f, fox_w_out,
              dil_w_in, dil_w_out, diff_w_in, diff_lambda, diff_subln, diff_w_out,
              sgu_w_in, sgu_norm_v, sgu_w_s, sgu_b_s, sgu_w_out, norm_final):
    pos = jnp.arange(x.shape[1], dtype=jnp.int32)
    h = x
    for i in range(DEPTH):
        kind, j = i % N_MIXERS, i // N_MIXERS
        h = h + 0.5 * swiglu(rms_norm(h, norm_ffn1[i]), w_ffn1_in[i], w_ffn1_out[i])
        hn = rms_norm(h, norm_mix[i])
        if kind == 0:
            y = fox_attention(hn, fox_w_in[j], fox_b_f[j], fox_w_out[j])
        elif kind == 1:
            y = dilated_attention(hn, dil_w_in[j], dil_w_out[j], pos)
        elif kind == 2:
            y = diff_attention(hn, diff_w_in[j], diff_lambda[j], diff_subln[j], diff_w_out[j], pos, i)
        else:
            y = chunked_sgu(hn, sgu_w_in[j], sgu_norm_v[j], sgu_w_s[j], sgu_b_s[j], sgu_w_out[j])
        h = h + y
        h = h + 0.5 * swiglu(rms_norm(h, norm_ffn2[i]), w_ffn2_in[i], w_ffn2_out[i])
        gate = jax.nn.sigmoid(rms_norm(h, norm_ple[i]) @ w_ple_gate[i] + b_ple_gate[i])
        h = h + gate * (p[i] @ w_ple_proj[i])
    return rms_norm(h, norm_final)
```

```python
import math
from contextlib import ExitStack, contextmanager
import numpy as np
import ml_dtypes
import concourse.bass as bass
import concourse.mybir as mybir
from concourse.bass_utils import run_bass_kernel_spmd

F32 = mybir.dt.float32
BF16 = mybir.dt.bfloat16
AF = mybir.ActivationFunctionType
ALU = mybir.AluOpType
AX = mybir.AxisListType

S = 4096
D = 1024
NT = 32
NB = 8
DFF = 2816
NEG = -30000.0
SKIP = set()


class Buf:
    def __init__(self, t):
        self.t = t
        self.w = None
        self.r = {}


class DSem:
    def __init__(self, sem):
        self.sem = sem
        self.cum = 0


class K:
    def __init__(self, nc, es):
        self.nc = nc
        self.es = es
        self.eng = {'pe': nc.tensor, 'act': nc.scalar, 'dve': nc.vector, 'pool': nc.gpsimd, 'sp': nc.sync}
        self.sem = {e: es.enter_context(nc.semaphore('s_' + e)) for e in self.eng}
        self.cnt = {e: 0 for e in self.eng}
        self.known = {e: {} for e in self.eng}
        self.dsems = [DSem(es.enter_context(nc.semaphore('d%d' % i))) for i in range(40)]
        self.dsems_sw = [DSem(es.enter_context(nc.semaphore('w%d' % i))) for i in range(24)]
        for d in self.dsems:
            d.kind = 'sp'
        for d in self.dsems_sw:
            d.kind = 'pool'
        self.dsi = 0
        self.dsi_sw = 0
        self.cur = es
        self.ps = [Buf(es.enter_context(nc.psum_tensor('ps%d' % i, [128, 512], F32))) for i in range(8)]
        self.uid = 0
        self.dq = []

    def defer(self, fns):
        self.dq.extend(fns)

    def pump(self, n=1):
        for _ in range(min(n, len(self.dq))):
            f = self.dq.pop(0)
            if f is not None:
                f()

    def flush(self, keep=0):
        while len(self.dq) > keep:
            f = self.dq.pop(0)
            if f is not None:
                f()

    def dsem(self, kind='sp'):
        if kind == 'pool':
            d = self.dsems_sw[self.dsi_sw % len(self.dsems_sw)]
            self.dsi_sw += 1
            return d
        d = self.dsems[self.dsi % len(self.dsems)]
        self.dsi += 1
        return d

    def _wait(self, e, deps):
        kn = self.known[e]
        for (sem, val) in deps:
            key = id(sem)
            if kn.get(key, 0) >= val:
                continue
            self.eng[e].wait_ge(sem, val)
            kn[key] = val

    def _deps(self, reads, writes):
        deps = []
        for t in reads:
            if t.w:
                deps.append(t.w)
        for t in writes:
            if t.w:
                deps.append(t.w)
            deps.extend(t.r.values())
        return deps

    def op(self, e, fn, reads=(), writes=()):
        deps = self._deps(reads, writes)
        own = self.sem[e]
        if e == 'pe':
            deps = [d for d in deps if d[0] is not own]
        self._wait(e, deps)
        inst = fn(self.eng[e])
        self.cnt[e] += 1
        inst.then_inc(own, 1)
        tok = (own, self.cnt[e])
        for t in reads:
            t.r[id(own)] = tok
        for t in writes:
            t.w = tok
            t.r = {}

    def dma(self, q, ds, pairs, reads=(), writes=()):
        assert ds.kind == q, (ds.kind, q)
        deps = self._deps(reads, writes)
        if ds.cum:
            deps.append((ds.sem, ds.cum))
        self._wait(q, deps)
        for (o, i) in pairs:
            self.eng[q].dma_start(out=o, in_=i).then_inc(ds.sem, 16)
            ds.cum += 16
        tok = (ds.sem, ds.cum)
        for t in reads:
            t.r[id(ds.sem)] = tok
        for t in writes:
            t.w = tok
            t.r = {}

    def barrier(self):
        deps = [(self.sem[e], self.cnt[e]) for e in self.eng if self.cnt[e]]
        deps += [(d.sem, d.cum) for d in self.dsems + self.dsems_sw if d.cum]
        for e in self.eng:
            self._wait(e, deps)

    @contextmanager
    def phase(self):
        prev = self.cur
        with ExitStack() as es:
            self.cur = es
            yield
            self.flush()
            self.barrier()
        self.cur = prev
        for b in self.ps:
            b.w = None
            b.r = {}

    def sb(self, name, shape, dt):
        self.uid += 1
        return Buf(self.cur.enter_context(self.nc.sbuf_tensor('%s_%d' % (name, self.uid), shape, dt)))

    def mm(self, pb, out, lhsT, rhs, start, stop, reads):
        self.op('pe', lambda e: e.matmul(out, lhsT, rhs, start=start, stop=stop), reads=reads, writes=[pb])

    def rstd_of(self, ssb, n_feat, eps, ms, rstd):
        self.op('pool', lambda e: e.tensor_scalar(out=ms.t[:], in0=ssb.t[:], scalar1=1.0 / n_feat, scalar2=eps,
                                                  op0=ALU.mult, op1=ALU.add), reads=[ssb], writes=[ms])
        np_, nf = ms.t.shape[0], ms.t.shape[1]
        self.op('pool', lambda e: e.tensor_tensor(out=rstd.t[:], in0=ms.t[:], in1=self.neghalf.t[0:np_, 0:nf],
                                                  op=ALU.pow), reads=[ms, self.neghalf], writes=[rstd])

    def alloc_norm(self, nh=1):
        st = {}
        st['hs'] = [self.sb('hs', [128, D], F32) for _ in range(4)]
        st['hsem'] = [self.dsem() for _ in range(4)]
        st['junk'] = [self.sb('junk', [128, D], BF16) for _ in range(2)]
        st['ss'] = [self.sb('ss', [128, 1], F32) for _ in range(4)]
        st['ms'] = [self.sb('ms', [128, 1], F32) for _ in range(4)]
        st['rstd'] = [self.sb('rstd', [128, 1], F32) for _ in range(4)]
        st['hnT'] = [[self.sb('hnT', [128, 512], BF16) for _ in range(8)] for _ in range(nh)]
        return st

    def norm_prep(self, st, b, hsrc):
        for i in range(4):
            ti = 4 * b + i
            hs = st['hs'][i]
            self.dma('sp', st['hsem'][i], [(hs.t[:], hsrc[0][ti * 128:(ti + 1) * 128, :])], reads=[hsrc[1][ti]], writes=[hs])
            jk = st['junk'][i % 2]
            ss, ms, rstd = st['ss'][i], st['ms'][i], st['rstd'][i]
            self.op('act', lambda e: e.activation(out=jk.t[:], in_=hs.t[:], func=AF.Square, accum_out=ss.t[:, 0:1]),
                    reads=[hs], writes=[jk, ss])
            self.rstd_of(ss, D, 1e-6, ms, rstd)
            self.op('dve', lambda e: e.tensor_scalar(out=hs.t[:], in0=hs.t[:], scalar1=rstd.t[:, 0:1], scalar2=None,
                                                     op0=ALU.mult), reads=[hs, rstd], writes=[hs])

    def norm_trans(self, st, gcol, hset=0, ps_ids=(6, 7)):
        for c in range(8):
            pt = self.ps[ps_ids[c % len(ps_ids)]]
            for i in range(4):
                hs = st['hs'][i]
                self.op('pe', lambda e: e.transpose(out=pt.t[:, i * 128:(i + 1) * 128], in_=hs.t[:, c * 128:(c + 1) * 128],
                                                    identity=self.ident.t[:]), reads=[hs, self.ident], writes=[pt])
            hn = st['hnT'][hset][c]
            g = self.gam.t[:, gcol + c:gcol + c + 1]
            if c % 2 == 0:
                self.op('act', lambda e: e.activation(out=hn.t[:], in_=pt.t[:], func=AF.Copy, scale=g),
                        reads=[pt, self.gam], writes=[hn])
            else:
                self.op('dve', lambda e: e.tensor_scalar(out=hn.t[:], in0=pt.t[:], scalar1=g, scalar2=None, op0=ALU.mult),
                        reads=[pt, self.gam], writes=[hn])
        return st['hnT'][hset]

    def load_w(self, W, wd, nk, parts=1):
        N = wd.shape[1]
        step = (N + parts - 1) // parts
        pairs = []
        for k in range(nk):
            for c0 in range(0, N, step):
                c1 = min(N, c0 + step)
                pairs.append((W.t[:, k, c0:c1], wd[k * 128:(k + 1) * 128, c0:c1]))
        self.dma('pool', self.dsem('pool'), pairs, writes=[W])

    def ffn_pass(self, w_in_d, w_out_d, gcol, hsrc, hdst):
        with self.phase():
            Win = self.sb('win', [128, 8, 2 * DFF], BF16)
            Wout = self.sb('wout', [128, 22, D], BF16)
            HF = 11 * 128
            WinC = [Buf(Win.t), Buf(Win.t)]
            WoutC = [Buf(Wout.t), Buf(Wout.t)]
            st = self.alloc_norm()
            for c_ in range(2):
                prs = []
                for k in range(8):
                    for base in (0, DFF):
                        prs.append((Win.t[:, k, base + c_ * HF:base + (c_ + 1) * HF], w_in_d[k * 128:(k + 1) * 128, base + c_ * HF:base + (c_ + 1) * HF]))
                self.dma('pool', self.dsem('pool'), prs, writes=[WinC[c_]])
                if c_ == 0:
                    self.norm_prep(st, 0, hsrc)
            for c_ in range(2):
                prs = [(Wout.t[:, j, :], w_out_d[j * 128:(j + 1) * 128, :]) for j in range(c_ * 11, (c_ + 1) * 11)]
                self.dma('pool', self.dsem('pool'), prs, writes=[WoutC[c_]])
            actT = [self.sb('actT', [128, 512], BF16) for _ in range(22)]
            sg = [self.sb('sg', [128, 512], F32) for _ in range(2)]
            hres = [self.sb('hres', [128, D], F32) for _ in range(2)]
            hrsem = [self.dsem() for _ in range(2)]
            hssem = [self.dsem('pool') for _ in range(2)]
            hnT = self.norm_trans(st, gcol)
            for b in range(NB):
                if b + 1 < NB:
                    self.norm_prep(st, b + 1, hsrc)
                for j in range(22):
                    pg, pu = self.ps[(2 * j) % 4], self.ps[(2 * j + 1) % 4]
                    for k in range(8):
                        self.mm(pg, pg.t[:, :], Win.t[:, k, j * 128:(j + 1) * 128], hnT[k].t[:, :], k == 0, k == 7, [WinC[j // 11], hnT[k]])
                    for k in range(8):
                        self.mm(pu, pu.t[:, :], Win.t[:, k, DFF + j * 128:DFF + (j + 1) * 128], hnT[k].t[:, :], k == 0, k == 7, [WinC[j // 11], hnT[k]])
                    s_ = sg[j % 2]
                    a_ = actT[j]
                    self.op('act', lambda e: e.activation(out=s_.t[:], in_=pg.t[:], func=AF.Silu), reads=[pg], writes=[s_])
                    self.op('dve', lambda e: e.tensor_tensor(out=a_.t[:], in0=pu.t[:], in1=s_.t[:], op=ALU.mult),
                            reads=[pu, s_], writes=[a_])
                if b + 1 < NB:
                    self.norm_trans(st, gcol)
                for i in range(4):
                    ti = 4 * b + i
                    hr = hres[i % 2]
                    self.dma('sp', hrsem[i % 2], [(hr.t[:], hsrc[0][ti * 128:(ti + 1) * 128, :])], reads=[hsrc[1][ti]], writes=[hr])
                    for half in range(2):
                        po = self.ps[4 + half]
                        for j in range(22):
                            self.mm(po, po.t[:, :], actT[j].t[:, i * 128:(i + 1) * 128], Wout.t[:, j, half * 512:(half + 1) * 512],
                                    j == 0, j == 21, [actT[j], WoutC[j // 11]])
                        self.op('dve', lambda e: e.scalar_tensor_tensor(out=hr.t[:, half * 512:(half + 1) * 512], in0=po.t[:], scalar=0.5,
                                                                        in1=hr.t[:, half * 512:(half + 1) * 512], op0=ALU.mult, op1=ALU.add),
                                reads=[po, hr], writes=[hr])
                    self.dma('pool', hssem[i % 2], [(hdst[0][ti * 128:(ti + 1) * 128, :], hr.t[:])], reads=[hr], writes=[hdst[1][ti]])

    def ple_pass(self, L, gcol, hsrc, hdst, final_out=None):
        I = self.inp
        with self.phase():
            Wg = self.sb('wg', [128, 8, D], BF16)
            Wp = self.sb('wp', [128, 2, D], BF16)
            self.load_w(Wg, I['w_ple_gate'][L], 8)
            self.load_w(Wp, I['w_ple_proj'][L], 2)
            bg = self.sb('bg', [128, D], F32)
            self.op('pool', lambda e: e.memset(bg.t[:], 0.0), writes=[bg])
            self.dma('sp', self.dsem(), [(bg.t[0:1, :], I['b_ple_gate'][L:L + 1, :])], writes=[bg])
            st = self.alloc_norm(2)
            pl = [self.sb('pl', [128, 256], F32) for _ in range(2)]
            plsem = [self.dsem() for _ in range(2)]
            pT = [self.sb('pT', [128, 256], BF16) for _ in range(2)]
            gate = [self.sb('gate', [128, 512], F32) for _ in range(2)]
            hres = [self.sb('hres', [128, D], F32) for _ in range(4)]
            hrsem = [self.dsem() for _ in range(4)]
            hssem = [self.dsem('pool') for _ in range(4)]
            if final_out is not None:
                gf = self.sb('gf', [128, D], F32)
                self.dma('sp', self.dsem(), [(gf.t[:], I['norm_final'][0:1, :].broadcast_to([128, D]))], writes=[gf])
                fjk = [self.sb('fjk', [128, D], BF16) for _ in range(2)]
                fss = [self.sb('fss', [128, 1], F32) for _ in range(2)]
                fms = [self.sb('fms', [128, 1], F32) for _ in range(2)]
                frs = [self.sb('frs', [128, 1], F32) for _ in range(2)]
            n = 0
            self.norm_prep(st, 0, hsrc)
            self.norm_trans(st, gcol, 0)
            for b in range(NB):
                hnT = st['hnT'][b % 2]
                if b + 1 < NB:
                    self.norm_prep(st, b + 1, hsrc)
                for i in range(4):
                    if i == 1 and b + 1 < NB:
                        self.norm_trans(st, gcol, (b + 1) % 2)
                    ti = 4 * b + i
                    p_, pt_, hr = pl[i % 2], pT[i % 2], hres[i % 4]
                    self.dma('sp', plsem[i % 2], [(p_.t[:], I['p'][L, ti * 128:(ti + 1) * 128, :])], writes=[p_])
                    self.dma('sp', hrsem[i % 4], [(hr.t[:], hsrc[0][ti * 128:(ti + 1) * 128, :])], reads=[hsrc[1][ti]], writes=[hr])
                    pp = self.ps[4]
                    for kc in range(2):
                        self.op('pe', lambda e: e.transpose(out=pp.t[:, kc * 128:(kc + 1) * 128], in_=p_.t[:, kc * 128:(kc + 1) * 128],
                                                            identity=self.ident.t[:]), reads=[p_, self.ident], writes=[pp])
                    self.op('act', lambda e: e.activation(out=pt_.t[:], in_=pp.t[:, 0:256], func=AF.Copy), reads=[pp], writes=[pt_])
                    for half in range(2):
                        hsl = slice(half * 512, (half + 1) * 512)
                        pg = self.ps[n % 2]
                        pq = self.ps[2 + n % 2]
                        g_ = gate[n % 2]
                        n += 1
                        for k in range(8):
                            self.mm(pg, pg.t[:, :], hnT[k].t[:, i * 128:(i + 1) * 128], Wg.t[:, k, hsl], k == 0, False, [hnT[k], Wg])
                        self.mm(pg, pg.t[:, :], self.ones0.t[:, 0:128], bg.t[:, hsl], False, True, [self.ones0, bg])
                        for kc in range(2):
                            self.mm(pq, pq.t[:, :], pt_.t[:, kc * 128:(kc + 1) * 128], Wp.t[:, kc, hsl], kc == 0, kc == 1, [pt_, Wp])
                        self.op('act', lambda e: e.activation(out=g_.t[:], in_=pg.t[:], func=AF.Sigmoid), reads=[pg], writes=[g_])
                        self.op('dve', lambda e: e.tensor_tensor(out=g_.t[:], in0=pq.t[:], in1=g_.t[:], op=ALU.mult), reads=[pq, g_], writes=[g_])
                        self.op('dve', lambda e: e.tensor_tensor(out=hr.t[:, hsl], in0=g_.t[:], in1=hr.t[:, hsl], op=ALU.add), reads=[g_, hr], writes=[hr])
                    if final_out is None:
                        self.dma('pool', hssem[i % 4], [(hdst[0][ti * 128:(ti + 1) * 128, :], hr.t[:])], reads=[hr], writes=[hdst[1][ti]])
                    else:
                        j_, s_, m_, r_ = fjk[i % 2], fss[i % 2], fms[i % 2], frs[i % 2]
                        self.op('act', lambda e: e.activation(out=j_.t[:], in_=hr.t[:], func=AF.Square, accum_out=s_.t[:, 0:1]), reads=[hr], writes=[j_, s_])
                        self.rstd_of(s_, D, 1e-6, m_, r_)
                        self.op('dve', lambda e: e.scalar_tensor_tensor(out=hr.t[:], in0=hr.t[:], scalar=r_.t[:, 0:1], in1=gf.t[:], op0=ALU.mult, op1=ALU.mult),
                                reads=[hr, r_, gf], writes=[hr])
                        self.dma('pool', hssem[i % 4], [(final_out[ti * 128:(ti + 1) * 128, :], hr.t[:])], reads=[hr], writes=[self.outtok])

    def final_pass(self, hsrc, out_d):
        I = self.inp
        with self.phase():
            gf = self.sb('gf', [128, D], F32)
            self.dma('sp', self.dsem(), [(gf.t[:], I['norm_final'][0:1, :].broadcast_to([128, D]))], writes=[gf])
            hs = [self.sb('fh', [128, D], F32) for _ in range(3)]
            hsem = [self.dsem() for _ in range(3)]
            fssem = [self.dsem('pool') for _ in range(3)]
            jk = [self.sb('fj', [128, D], BF16) for _ in range(2)]
            ss = [self.sb('fss', [128, 1], F32) for _ in range(3)]
            ms = [self.sb('fms', [128, 1], F32) for _ in range(3)]
            rs = [self.sb('frs', [128, 1], F32) for _ in range(3)]
            for ti in range(NT):
                h_, s_, m_, r_, j_ = hs[ti % 3], ss[ti % 3], ms[ti % 3], rs[ti % 3], jk[ti % 2]
                self.dma('sp', hsem[ti % 3], [(h_.t[:], hsrc[0][ti * 128:(ti + 1) * 128, :])], reads=[hsrc[1][ti]], writes=[h_])
                self.op('act', lambda e: e.activation(out=j_.t[:], in_=h_.t[:], func=AF.Square, accum_out=s_.t[:, 0:1]), reads=[h_], writes=[j_, s_])
                self.rstd_of(s_, D, 1e-6, m_, r_)
                self.op('dve', lambda e: e.scalar_tensor_tensor(out=h_.t[:], in0=h_.t[:], scalar=r_.t[:, 0:1], in1=gf.t[:], op0=ALU.mult, op1=ALU.mult),
                        reads=[h_, r_, gf], writes=[h_])
                self.dma('pool', fssem[ti % 3], [(out_d[ti * 128:(ti + 1) * 128, :], h_.t[:])], reads=[h_], writes=[self.outtok])


    def rope(self, q, ti, tmp):
        v = q.t[:, :].rearrange("p (h d) -> p h d", d=64)
        x1, x2 = v[:, :, 0:8], v[:, :, 8:16]
        cos = self.cs.t[:, ti * 16:ti * 16 + 8].unsqueeze(1).broadcast_to([128, 8, 8])
        sin = self.cs.t[:, ti * 16 + 8:ti * 16 + 16].unsqueeze(1).broadcast_to([128, 8, 8])
        t = [b.t[:, :].rearrange("p (h d) -> p h d", d=8) for b in tmp]
        self.op('dve', lambda e: e.tensor_tensor(out=t[0], in0=x1, in1=cos, op=ALU.mult), reads=[q, self.cs], writes=[tmp[0]])
        self.op('dve', lambda e: e.tensor_tensor(out=t[1], in0=x2, in1=sin, op=ALU.mult), reads=[q, self.cs], writes=[tmp[1]])
        self.op('pool', lambda e: e.tensor_tensor(out=t[2], in0=x1, in1=sin, op=ALU.mult), reads=[q, self.cs], writes=[tmp[2]])
        self.op('pool', lambda e: e.tensor_tensor(out=t[3], in0=x2, in1=cos, op=ALU.mult), reads=[q, self.cs], writes=[tmp[3]])
        self.op('dve', lambda e: e.tensor_tensor(out=x1, in0=t[0], in1=t[1], op=ALU.subtract), reads=[tmp[0], tmp[1], tmp[2], tmp[3]], writes=[q])
        self.op('pool', lambda e: e.tensor_tensor(out=x2, in0=t[2], in1=t[3], op=ALU.add), reads=[tmp[2], tmp[3]], writes=[q])

    def proj_pass(self, wd, NF, gcol, hsrc, groups, fox=False, sgu=None):
        if 'proj' in SKIP:
            return
        I = self.inp
        with self.phase():
            W = self.sb('pw', [128, 8, NF], BF16)
            self.load_w(W, wd, 8)
            st = self.alloc_norm(2)
            qs = [self.sb('qs', [128, 512], F32) for _ in range(8)]
            tmp = [[self.sb('rt', [128, 64], F32) for _ in range(4)] for _ in range(2)]
            qst = [self.sb('qst', [128, 512], BF16) for _ in range(2)]
            qsem = [self.dsem() for _ in range(2)]
            vst = [self.sb('vst', [128, 512], BF16) for _ in range(2)]
            vsem = [self.dsem() for _ in range(2)]
            if fox:
                bfb = self.sb('bfb', [128, 16], F32)
                self.dma('sp', self.dsem(), [(bfb.t[:], I['fox_b_f'][0:1, :].broadcast_to([128, 16]))], writes=[bfb])
                LF = self.sb('LF', [128, 512], F32)
                xg = [self.sb('xg', [128, 16], F32) for _ in range(2)]
            nq = 0
            nv = 0
            pending = None
            self.norm_prep(st, 0, hsrc)
            self.norm_trans(st, gcol, 0)
            for b in range(NB):
                hnT = st['hnT'][b % 2]
                if b + 1 < NB:
                    self.norm_prep(st, b + 1, hsrc)
                for gi, (kind, c0, dst, d0, scale, rp) in enumerate(groups):
                    if gi == 1 and b + 1 < NB:
                        self.norm_trans(st, gcol, (b + 1) % 2)
                    for i in range(4):
                        ti = 4 * b + i
                        pm = self.ps[i]
                        for k in range(8):
                            self.mm(pm, pm.t[:, :], hnT[k].t[:, i * 128:(i + 1) * 128], W.t[:, k, c0:c0 + 512], k == 0, k == 7, [hnT[k], W])
                        if kind == 'N':
                            v_ = vst[nv % 2]
                            self.op('act', lambda e: e.activation(out=v_.t[:], in_=pm.t[:], func=AF.Copy), reads=[pm], writes=[v_])
                            self.dma('sp', vsem[nv % 2], [(dst[ti * 128:(ti + 1) * 128, d0:d0 + 512], v_.t[:])], reads=[v_])
                            nv += 1
                        else:
                            q_ = qs[(gi % 2) * 4 + i]
                            self.op('act', lambda e: e.activation(out=q_.t[:], in_=pm.t[:], func=AF.Copy), reads=[pm], writes=[q_])
                            if rp:
                                self.rope(q_, ti, tmp[i % 2])
                    if pending is not None:
                        pending()
                        pending = None
                    if kind == 'T':
                        def _tr(gi=gi, dst=dst, d0=d0, scale=scale, b=b):
                            nonlocal nq
                            for fc in range(4):
                                pt = self.ps[4 + fc % 2]
                                for i in range(4):
                                    q_ = qs[(gi % 2) * 4 + i]
                                    self.op('pe', lambda e: e.transpose(out=pt.t[:, i * 128:(i + 1) * 128], in_=q_.t[:, fc * 128:(fc + 1) * 128],
                                                                        identity=self.ident.t[:]), reads=[q_, self.ident], writes=[pt])
                                s_ = qst[nq % 2]
                                self.op('dve', lambda e: e.tensor_scalar(out=s_.t[:], in0=pt.t[:], scalar1=float(scale), scalar2=None, op0=ALU.mult),
                                        reads=[pt], writes=[s_])
                                self.dma('sp', qsem[nq % 2], [(dst[d0 + fc * 128:d0 + (fc + 1) * 128, b * 512:(b + 1) * 512], s_.t[:])], reads=[s_])
                                nq += 1
                        pending = _tr
                if pending is not None:
                    pending()
                    pending = None
                if fox:
                    for i in range(4):
                        ti = 4 * b + i
                        pm = self.ps[i]
                        x_ = xg[i % 2]
                        for k in range(8):
                            self.mm(pm, pm.t[:, 0:16], hnT[k].t[:, i * 128:(i + 1) * 128], W.t[:, k, 3072:3088], k == 0, k == 7, [hnT[k], W])
                        self.op('dve', lambda e: e.tensor_tensor(out=x_.t[:], in0=pm.t[:, 0:16], in1=bfb.t[:], op=ALU.add), reads=[pm, bfb], writes=[x_])
                        self.op('act', lambda e: e.activation(out=x_.t[:], in_=x_.t[:], func=AF.Exp, scale=-1.0), reads=[x_], writes=[x_])
                        self.op('act', lambda e: e.activation(out=LF.t[:, ti * 16:(ti + 1) * 16], in_=x_.t[:], func=AF.Ln, bias=self.ones32.t[:, 0:1]),
                                reads=[x_, self.ones32], writes=[LF])
            if fox:
                self.fox_cumsum(LF)

    def fox_cumsum(self, LF):
        C3d = self.scr['C3'][0]
        w1, tot = self.ps[0], self.ps[1]
        self.mm(w1, w1.t[:, :], self.mk.t[:, 640:768], LF.t[:, :], True, True, [self.mk, LF])
        self.mm(tot, tot.t[:, :], self.ones32.t[:, :], LF.t[:, :], True, True, [self.ones32, LF])
        tots = self.sb('tots', [128, 512], F32)
        pre = self.sb('pre', [128, 512], F32)
        self.op('act', lambda e: e.activation(out=tots.t[:], in_=tot.t[:], func=AF.Copy), reads=[tot], writes=[tots])
        self.op('dve', lambda e: e.memset(pre.t[:, 0:16], 0.0), writes=[pre])
        for i in range(1, NT):
            self.op('dve', lambda e: e.tensor_tensor(out=pre.t[:, i * 16:(i + 1) * 16], in0=pre.t[:, (i - 1) * 16:i * 16],
                                                     in1=tots.t[:, (i - 1) * 16:i * 16], op=ALU.add), reads=[pre, tots], writes=[pre])
        cn = self.cneg
        self.op('dve', lambda e: e.tensor_tensor(out=cn.t[:], in0=w1.t[:], in1=pre.t[:], op=ALU.add), reads=[w1, pre], writes=[cn])
        c3 = self.sb('c3', [128, NT, 48], F32)
        pb = self.sb('pb', [128, 512], BF16)
        r1 = self.sb('r1', [128, 512], F32)
        r2 = self.sb('r2', [128, 512], F32)
        v3 = lambda a: a.rearrange("p (i h) -> p i h", h=16)
        self.op('dve', lambda e: e.tensor_scalar(out=pb.t[:], in0=cn.t[:], scalar1=-1.0, scalar2=None, op0=ALU.mult), reads=[cn], writes=[pb])
        self.op('dve', lambda e: e.tensor_copy(out=c3.t[:, :, 0:16], in_=v3(pb.t[:, :])), reads=[pb], writes=[c3])
        self.op('dve', lambda e: e.scalar_tensor_tensor(out=r1.t[:], in0=cn.t[:], scalar=-1.0, in1=pb.t[:], op0=ALU.mult, op1=ALU.subtract),
                reads=[cn, pb], writes=[r1])
        self.op('dve', lambda e: e.tensor_copy(out=pb.t[:], in_=r1.t[:]), reads=[r1, c3], writes=[pb])
        self.op('dve', lambda e: e.tensor_copy(out=c3.t[:, :, 16:32], in_=v3(pb.t[:, :])), reads=[pb], writes=[c3])
        self.op('dve', lambda e: e.tensor_tensor(out=r2.t[:], in0=r1.t[:], in1=pb.t[:], op=ALU.subtract), reads=[r1, pb], writes=[r2])
        self.op('dve', lambda e: e.tensor_copy(out=pb.t[:], in_=r2.t[:]), reads=[r2, c3], writes=[pb])
        self.op('dve', lambda e: e.tensor_copy(out=c3.t[:, :, 32:48], in_=v3(pb.t[:, :])), reads=[pb], writes=[c3])
        cst = [self.sb('cst', [48, 512], BF16) for _ in range(2)]
        csem = [self.dsem() for _ in range(2)]
        for b in range(NB):
            pt = self.ps[2 + b % 2]
            for i in range(4):
                self.op('pe', lambda e: e.transpose(out=pt.t[0:48, i * 128:(i + 1) * 128], in_=c3.t[:, 4 * b + i, :], identity=self.ident.t[:]),
                        reads=[c3, self.ident], writes=[pt])
            s_ = cst[b % 2]
            self.op('act', lambda e: e.activation(out=s_.t[:], in_=pt.t[0:48, :], func=AF.Copy), reads=[pt], writes=[s_])
            self.dma('sp', csem[b % 2], [(C3d[:, b * 512:(b + 1) * 512], s_.t[:])], reads=[s_])

    def bcast_row64(self, rb, row, n=512):
        self.mm(rb, rb.t[0:65, 0:n], self.sel64.t[:, 0:65], row.t[0:128, 0:n], True, True, [self.sel64, row])

    def fox_attn(self):
        QT, KT, V, OT, C3 = [self.scr[n][0] for n in ['QT', 'KT', 'V', 'OT', 'C3']]
        Vv = V.rearrange("(kc p) f -> p kc f", p=128)
        C3v = C3.rearrange("(a h) s -> h a s", h=16)
        with self.phase():
            QA = [self.sb('QA', [67, S], BF16) for _ in range(2)]
            KA = [self.sb('KA', [67, S], BF16) for _ in range(2)]
            VA = [self.sb('VA', [128, NT, 65], BF16) for _ in range(2)]
            lsem = [[self.dsem() for _ in range(3)] for _ in range(2)]
            for x in range(2):
                self.op('pool', lambda e: e.memset(KA[x].t[64:67, :], 1.0), writes=[KA[x]])
                self.op('pool', lambda e: e.memset(VA[x].t[:, :, 64:65], 1.0), writes=[VA[x]])
            NSB, NPT, LA = 4, 6, 3
            pts = [self.sb('pt', [128, 512], BF16) for _ in range(NPT)]
            oa = [self.sb('oa', [65, 512], F32) for _ in range(2)]
            lr = [self.sb('lr', [128, 512], F32) for _ in range(2)]
            for b_ in lr:
                self.op('pool', lambda e: e.memset(b_.t[:], 0.0), writes=[b_])
            on = [self.sb('on', [64, 512], BF16) for _ in range(2)]
            osem = [self.dsem('pool') for _ in range(2)]
            lsc = [self.sb('lsc', [65, 512], F32) for _ in range(2)]
            jobs = [(h, qb, kc) for h in range(16) for qb in range(NB) for kc in range(4 * qb + 4)]

            def stage_a(i):
                h, qb, kc = jobs[i]
                x = h % 2
                if qb == 0 and kc == 0:
                    self.dma('sp', lsem[x][0], [(QA[x].t[0:64, :], QT[h * 64:(h + 1) * 64, :]), (QA[x].t[64:67, :], C3v[h])], writes=[QA[x]])
                    self.dma('sp', lsem[x][1], [(KA[x].t[0:64, :], KT[h * 64:(h + 1) * 64, :])], writes=[KA[x]])
                    self.dma('sp', lsem[x][2], [(VA[x].t[:, :, 0:64], Vv[:, :, h * 64:(h + 1) * 64])], writes=[VA[x]])
                j = kc - 4 * qb
                q0 = max(j, 0) * 128
                ps_ = self.ps[i % NSB]
                pt_ = pts[i % NPT]
                self.mm(ps_, ps_.t[:, q0:512], KA[x].t[0:67, kc * 128:(kc + 1) * 128], QA[x].t[0:67, qb * 512 + q0:(qb + 1) * 512],
                        True, True, [KA[x], QA[x]])
                if j >= 0:
                    self.op('dve', lambda e: e.tensor_tensor(out=ps_.t[:, q0:q0 + 128], in0=ps_.t[:, q0:q0 + 128], in1=self.mk.t[:, 0:128], op=ALU.add),
                            reads=[self.mk], writes=[ps_])
                self.op('act', lambda e: e.activation(out=pt_.t[:, q0:512], in_=ps_.t[:, q0:512], func=AF.Exp,
                                                      bias=self.cneg.t[:, kc * 16 + h:kc * 16 + h + 1]), reads=[ps_, self.cneg], writes=[pt_])

            def stage_b(i):
                h, qb, kc = jobs[i]
                x = h % 2
                nf = h * NB + qb
                po = self.ps[4 + nf % 2]
                nk = 4 * qb + 4
                q0 = max(kc - 4 * qb, 0) * 128
                pt_ = pts[i % NPT]
                self.mm(po, po.t[0:65, q0:512], VA[x].t[:, kc, 0:65], pt_.t[:, q0:512], kc == 0, kc == nk - 1, [VA[x], pt_])
                if kc == nk - 1:
                    self.flush(keep=11)
                    self.fox_fin(po, oa[nf % 2], lr[nf % 2], on[nf % 2], osem[nf % 2], self.ps[6 + nf % 2], OT, h, qb, lsc[nf % 2])
                self.pump(1)

            for i in range(len(jobs) + LA):
                if i < len(jobs):
                    stage_a(i)
                if i - LA >= 0:
                    stage_b(i - LA)


    def fox_fin(self, po, oa_, lr_, on_, osem_, rb, OT, h, qb, sc_):
        self.op('dve', lambda e: e.tensor_copy(out=oa_.t[:, :], in_=po.t[0:65, :]), reads=[po], writes=[oa_])
        self.defer([
            lambda: self.op('dve', lambda e: e.reciprocal(out=lr_.t[64:65, :], in_=oa_.t[64:65, :]), reads=[oa_], writes=[lr_]),
            None, None, None, None, None, None,
            lambda: self.bcast_row64(rb, lr_),
            None,
            lambda: self.op('dve', lambda e: e.tensor_tensor(out=on_.t[:, :], in0=oa_.t[0:64, :], in1=rb.t[0:64, :], op=ALU.mult), reads=[oa_, rb], writes=[on_]),
            lambda: self.dma('pool', osem_, [(OT[h * 64:(h + 1) * 64, qb * 512:(qb + 1) * 512], on_.t[:, :])], reads=[on_]),
        ])

    def oproj_pass(self, w_out_d, nk, hsrc, hdst):
        if 'oproj' in SKIP:
            return
        OT = self.scr['OT'][0]
        OTv = OT.rearrange("(c p) s -> p c s", p=128)
        with self.phase():
            Wo = self.sb('wo', [128, nk, D], BF16)
            self.load_w(Wo, w_out_d, nk)
            ot = [self.sb('ot', [128, nk, 512], BF16) for _ in range(2)]
            otsem = [self.dsem() for _ in range(2)]
            hres = [self.sb('hres', [128, D], F32) for _ in range(4)]
            hrsem = [self.dsem() for _ in range(4)]
            hssem = [self.dsem('pool') for _ in range(4)]
            for b in range(NB):
                o_ = ot[b % 2]
                self.dma('sp', otsem[b % 2], [(o_.t[:, :, :], OTv[:, 0:nk, b * 512:(b + 1) * 512])], writes=[o_])
                for i in range(4):
                    ti = 4 * b + i
                    hr = hres[i % 4]
                    self.dma('sp', hrsem[i % 4], [(hr.t[:], hsrc[0][ti * 128:(ti + 1) * 128, :])], reads=[hsrc[1][ti]], writes=[hr])
                    for half in range(2):
                        hsl = slice(half * 512, (half + 1) * 512)
                        po = self.ps[(2 * i + half) % 4]
                        for c in range(nk):
                            self.mm(po, po.t[:, :], o_.t[:, c, i * 128:(i + 1) * 128], Wo.t[:, c, hsl], c == 0, c == nk - 1, [o_, Wo])
                        self.op('dve', lambda e: e.tensor_tensor(out=hr.t[:, hsl], in0=po.t[:], in1=hr.t[:, hsl], op=ALU.add), reads=[po, hr], writes=[hr])
                    self.dma('pool', hssem[i % 4], [(hdst[0][ti * 128:(ti + 1) * 128, :], hr.t[:])], reads=[hr], writes=[hdst[1][ti]])

    def fox_mixer(self, L, gcol, hsrc, hdst):
        I = self.inp
        QT, KT, V = self.scr['QT'][0], self.scr['KT'][0], self.scr['V'][0]
        with ExitStack() as es:
            prev, self.cur = self.cur, es
            self.cneg = self.sb('cneg', [128, 512], F32)
            self.cur = prev
            groups = [('T', 0, QT, 0, 0.125, False), ('T', 512, QT, 512, 0.125, False),
                      ('T', 1024, KT, 0, 1.0, False), ('T', 1536, KT, 512, 1.0, False),
                      ('N', 2048, V, 0, 1.0, False), ('N', 2560, V, 512, 1.0, False)]
            self.proj_pass(I['fox_w_in'], 3088, gcol, hsrc, groups, fox=True)
            self.fox_attn()
            self.barrier()
        self.oproj_pass(I['fox_w_out'], 8, hsrc, hdst)


    def diff_mixer(self, L, gcol, hsrc, hdst):
        I = self.inp
        QT, KT, V, OT = [self.scr[n][0] for n in ['QT', 'KT', 'V', 'OT']]
        groups = [('T', 0, QT, 0, 0.125, True), ('T', 512, QT, 512, 0.125, True),
                  ('T', 1024, KT, 0, 1.0, True), ('T', 1536, KT, 512, 1.0, True),
                  ('N', 2048, V, 0, 1.0, False), ('N', 2560, V, 512, 1.0, False)]
        self.proj_pass(I['diff_w_in'], 3072, gcol, hsrc, groups)
        lam_init = 0.8 - 0.6 * math.exp(-0.3 * L)
        Vv = V.rearrange("(kc p) f -> p kc f", p=128)
        with self.phase():
            lp = self.sb('lp', [65, 256], F32)
            lt = self.sb('lt', [65, 128], F32)
            l2 = self.sb('l2', [65, 4], F32)
            self.dma('sp', self.dsem(), [(lp.t[64:65, :], I['diff_lambda'][0:1, :])], writes=[lp])
            self.op('dve', lambda e: e.tensor_tensor(out=lt.t[64:65, 0:64], in0=lp.t[64:65, 0:64], in1=lp.t[64:65, 64:128], op=ALU.mult), reads=[lp], writes=[lt])
            self.op('dve', lambda e: e.tensor_tensor(out=lt.t[64:65, 64:128], in0=lp.t[64:65, 128:192], in1=lp.t[64:65, 192:256], op=ALU.mult), reads=[lp], writes=[lt])
            self.op('dve', lambda e: e.reduce_sum(out=l2.t[64:65, 0:1], in_=lt.t[64:65, 0:64], axis=AX.X), reads=[lt], writes=[l2])
            self.op('dve', lambda e: e.reduce_sum(out=l2.t[64:65, 1:2], in_=lt.t[64:65, 64:128], axis=AX.X), reads=[lt], writes=[l2])
            self.op('act', lambda e: e.activation(out=l2.t[64:65, 0:2], in_=l2.t[64:65, 0:2], func=AF.Exp), reads=[l2], writes=[l2])
            self.op('dve', lambda e: e.tensor_tensor(out=l2.t[64:65, 2:3], in0=l2.t[64:65, 1:2], in1=l2.t[64:65, 0:1], op=ALU.subtract), reads=[l2], writes=[l2])
            self.op('dve', lambda e: e.tensor_scalar(out=l2.t[64:65, 3:4], in0=l2.t[64:65, 2:3], scalar1=-lam_init, scalar2=None, op0=ALU.add), reads=[l2], writes=[l2])
            gs = self.sb('gs', [64, 2], F32)
            self.dma('sp', self.dsem(), [(gs.t[:, 0:1], I['diff_subln'][0:64, :]), (gs.t[:, 1:2], I['diff_subln'][64:128, :])], writes=[gs])
            self.op('dve', lambda e: e.tensor_scalar(out=gs.t[:], in0=gs.t[:], scalar1=1.0 - lam_init, scalar2=None, op0=ALU.mult), reads=[gs], writes=[gs])
            QA = [self.sb('QA', [128, S], BF16) for _ in range(2)]
            KA = [self.sb('KA', [128, S], BF16) for _ in range(2)]
            VA = [self.sb('VA', [128, NT, 130], BF16) for _ in range(2)]
            lsem = [[self.dsem() for _ in range(2)] for _ in range(2)]
            vsem = [self.dsem() for _ in range(2)]
            for x in range(2):
                self.op('pool', lambda e: e.memset(VA[x].t[:, :, 64:65], 1.0), writes=[VA[x]])
                self.op('pool', lambda e: e.memset(VA[x].t[:, :, 129:130], 0.0), writes=[VA[x]])
                self.op('pool', lambda e: e.memset(QA[x].t[64:128, :], 0.0), writes=[QA[x]])
                self.op('pool', lambda e: e.memset(KA[x].t[64:128, :], 0.0), writes=[KA[x]])
            NSB, NPT, LA = 3, 5, 2
            pts = [self.sb('pt', [128, 512], BF16) for _ in range(NPT)]
            oa0 = self.sb('oa0', [65, S], F32)
            ob0 = self.sb('ob0', [64, S], F32)
            oa1 = [self.sb('oa1', [65, 512], F32) for _ in range(2)]
            ob1 = [self.sb('ob1', [64, 512], F32) for _ in range(2)]
            lr = [self.sb('lr', [128, 1024], F32) for _ in range(2)]
            ta = [self.sb('ta', [64, 512], F32) for _ in range(2)]
            tb = [self.sb('tb', [64, 512], F32) for _ in range(2)]
            sq = [self.sb('sq', [128, 1024], F32) for _ in range(2)]
            for b_ in lr + sq:
                self.op('pool', lambda e: e.memset(b_.t[:], 0.0), writes=[b_])
            msb = [self.sb('msb', [64, 512], F32) for _ in range(2)]
            rsb = [self.sb('rsb', [64, 512], F32) for _ in range(2)]
            on = [self.sb('on', [64, 1024], BF16) for _ in range(2)]
            osem = [self.dsem('pool') for _ in range(2)]
            eps5 = self.sb('eps5', [64, 1], F32)
            self.op('pool', lambda e: e.memset(eps5.t[:], 1e-5), writes=[eps5])
            rsc = [self.sb('rsc', [65, 1024], F32) for _ in range(2)]
            jobs = [(h, m, qb, kc) for h in range(8) for m in range(2) for qb in range(NB) for kc in range(4 * qb + 4)]

            def stage_a(i):
                h, m, qb, kc = jobs[i]
                xv = h % 2
                n = h * 2 + m
                x = n % 2
                if qb == 0 and kc == 0:
                    if m == 0:
                        self.dma('sp', vsem[xv], [(VA[xv].t[:, :, 0:64], Vv[:, :, h * 128:h * 128 + 64]),
                                                  (VA[xv].t[:, :, 65:129], Vv[:, :, h * 128 + 64:h * 128 + 128])], writes=[VA[xv]])
                    self.dma('sp', lsem[x][0], [(QA[x].t[0:64, :], QT[n * 64:(n + 1) * 64, :])], writes=[QA[x]])
                    self.dma('sp', lsem[x][1], [(KA[x].t[0:64, :], KT[n * 64:(n + 1) * 64, :])], writes=[KA[x]])
                j = kc - 4 * qb
                q0 = max(j, 0) * 128
                ps_ = self.ps[i % NSB]
                pt_ = pts[i % NPT]
                self.mm(ps_, ps_.t[:, q0:512], KA[x].t[0:128, kc * 128:(kc + 1) * 128], QA[x].t[0:128, qb * 512 + q0:(qb + 1) * 512],
                        True, True, [KA[x], QA[x]])
                if j >= 0:
                    self.op('dve', lambda e: e.tensor_tensor(out=ps_.t[:, q0:q0 + 128], in0=ps_.t[:, q0:q0 + 128], in1=self.mk.t[:, 0:128], op=ALU.add),
                            reads=[self.mk], writes=[ps_])
                self.op('act', lambda e: e.activation(out=pt_.t[:, q0:512], in_=ps_.t[:, q0:512], func=AF.Exp), reads=[ps_], writes=[pt_])

            def stage_b(i):
                h, m, qb, kc = jobs[i]
                xv = h % 2
                nq = (h * 2 + m) * NB + qb
                pa, pb = self.ps[3 + nq % 2], self.ps[5 + nq % 2]
                nk = 4 * qb + 4
                q0 = max(kc - 4 * qb, 0) * 128
                pt_ = pts[i % NPT]
                self.mm(pa, pa.t[0:65, q0:512], VA[xv].t[:, kc, 0:65], pt_.t[:, q0:512], kc == 0, kc == nk - 1, [VA[xv], pt_])
                self.mm(pb, pb.t[0:65, q0:512], VA[xv].t[:, kc, 65:130], pt_.t[:, q0:512], kc == 0, kc == nk - 1, [VA[xv], pt_])
                if kc == nk - 1:
                    qsl = slice(qb * 512, (qb + 1) * 512)
                    if m == 0:
                        self.op('dve', lambda e: e.tensor_copy(out=oa0.t[:, qsl], in_=pa.t[0:65, :]), reads=[pa], writes=[oa0])
                        self.op('dve', lambda e: e.tensor_copy(out=ob0.t[:, qsl], in_=pb.t[0:64, :]), reads=[pb], writes=[ob0])
                    else:
                        f = (h * NB + qb) % 2
                        self.flush(keep=44)
                        self.diff_fin(pa, pb, oa0, ob0, oa1[f], ob1[f], lr[f], ta[f], tb[f], sq[f], msb[f], rsb[f], on[f], osem[f], l2, gs, eps5, OT, h, qsl, rsc[f])
                self.pump(2)

            for i in range(len(jobs) + LA):
                if i < len(jobs):
                    stage_a(i)
                if i - LA >= 0:
                    stage_b(i - LA)

        self.oproj_pass(I['diff_w_out'], 8, hsrc, hdst)

    def diff_fin(self, pa, pb, oa0, ob0, a1, b1, lr_, ta_, tb_, sq_, ms_, rs_, on_, osem_, l2, gs, eps5, OT, h, qsl, sc_):
        op = self.op
        op('dve', lambda e: e.tensor_copy(out=a1.t[:, :], in_=pa.t[0:65, :]), reads=[pa], writes=[a1])
        op('dve', lambda e: e.tensor_copy(out=b1.t[:, :], in_=pb.t[0:64, :]), reads=[pb], writes=[b1])
        rr = self.ps[7]
        self.defer([
            lambda: op('dve', lambda e: e.reciprocal(out=lr_.t[64:65, 0:512], in_=oa0.t[64:65, qsl]), reads=[oa0], writes=[lr_]),
            None, None,
            lambda: op('dve', lambda e: e.reciprocal(out=lr_.t[64:65, 512:1024], in_=a1.t[64:65, :]), reads=[a1], writes=[lr_]),
            None, None,
            lambda: op('dve', lambda e: e.tensor_scalar(out=lr_.t[64:65, 512:1024], in0=lr_.t[64:65, 512:1024], scalar1=l2.t[64:65, 3:4], scalar2=None, op0=ALU.mult),
                       reads=[lr_, l2], writes=[lr_]),
            None, None, None, None,
            lambda: self.mm(rr, rr.t[0:65, :], self.sel64.t[:, 0:65], lr_.t[0:128, 0:512], True, True, [self.sel64, lr_]),
            None, None,
            lambda: op('dve', lambda e: e.tensor_tensor(out=ta_.t[:, :], in0=oa0.t[0:64, qsl], in1=rr.t[0:64, :], op=ALU.mult), reads=[oa0, rr], writes=[ta_]),
            lambda: op('dve', lambda e: e.tensor_tensor(out=tb_.t[:, :], in0=ob0.t[0:64, qsl], in1=rr.t[0:64, :], op=ALU.mult), reads=[ob0, rr], writes=[tb_]),
            None, None, None,
            lambda: self.mm(rr, rr.t[0:65, :], self.sel64.t[:, 0:65], lr_.t[0:128, 512:1024], True, True, [self.sel64, lr_]),
            None, None,
            lambda: op('dve', lambda e: e.tensor_tensor(out=a1.t[0:64, :], in0=a1.t[0:64, :], in1=rr.t[0:64, :], op=ALU.mult), reads=[rr], writes=[a1]),
            lambda: op('dve', lambda e: e.tensor_tensor(out=b1.t[0:64, :], in0=b1.t[0:64, :], in1=rr.t[0:64, :], op=ALU.mult), reads=[rr], writes=[b1]),
            lambda: op('pool', lambda e: e.tensor_tensor(out=ta_.t[:, :], in0=ta_.t[:, :], in1=a1.t[0:64, :], op=ALU.add), reads=[a1], writes=[ta_]),
            lambda: op('pool', lambda e: e.tensor_tensor(out=tb_.t[:, :], in0=tb_.t[:, :], in1=b1.t[0:64, :], op=ALU.add), reads=[b1], writes=[tb_]),
            lambda: op('act', lambda e: e.activation(out=sq_.t[0:64, 0:512], in_=ta_.t[:, :], func=AF.Square), reads=[ta_], writes=[sq_]),
            lambda: op('act', lambda e: e.activation(out=sq_.t[0:64, 512:1024], in_=tb_.t[:, :], func=AF.Square), reads=[tb_], writes=[sq_]),
            None, None, None,
            lambda: self.mm(rr, rr.t[0:65, :], self.onesT.t[:, 0:65], sq_.t[0:128, 0:512], True, False, [self.onesT, sq_]),
            lambda: self.mm(rr, rr.t[0:65, :], self.onesT.t[:, 0:65], sq_.t[0:128, 512:1024], False, True, [self.onesT, sq_]),
            lambda: op('act', lambda e: e.activation(out=ms_.t[:, :], in_=rr.t[0:64, :], func=AF.Sqrt, scale=1.0 / 128, bias=eps5.t[:, 0:1]), reads=[rr, eps5], writes=[ms_]),
            None, None,
            lambda: op('dve', lambda e: e.reciprocal(out=rs_.t[:, :], in_=ms_.t[:, :]), reads=[ms_], writes=[rs_]),
            None, None, None, None,
            lambda: op('dve', lambda e: e.scalar_tensor_tensor(out=on_.t[:, 0:512], in0=ta_.t[:, :], scalar=gs.t[:, 0:1], in1=rs_.t[:, :], op0=ALU.mult, op1=ALU.mult),
                       reads=[ta_, gs, rs_], writes=[on_]),
            lambda: op('dve', lambda e: e.scalar_tensor_tensor(out=on_.t[:, 512:1024], in0=tb_.t[:, :], scalar=gs.t[:, 1:2], in1=rs_.t[:, :], op0=ALU.mult, op1=ALU.mult),
                       reads=[tb_, gs, rs_], writes=[on_]),
            lambda: self.dma('pool', osem_, [(OT[h * 128:h * 128 + 64, qsl], on_.t[:, 0:512]), (OT[h * 128 + 64:h * 128 + 128, qsl], on_.t[:, 512:1024])], reads=[on_]),
        ])

    def dil_mixer(self, L, gcol, hsrc, hdst):
        I = self.inp
        QT, KT, V, OT = [self.scr[n][0] for n in ['QT', 'KT', 'V', 'OT']]
        groups = []
        for g in range(3):
            for hf in range(2):
                groups.append(('T', g * 1024 + hf * 512, QT, g * 1024 + hf * 512, 0.125, True))
        groups += [('T', 3072, KT, 0, 1.0, True), ('T', 3584, KT, 512, 1.0, True),
                   ('N', 4096, V, 0, 1.0, False), ('N', 4608, V, 512, 1.0, False)]
        self.proj_pass(I['dil_w_in'], 5120, gcol, hsrc, groups)
        dils = [1, 4, 16]
        with self.phase():
            Q3 = [self.sb('Q3', [128, 3, S], BF16) for _ in range(2)]
            KA = [self.sb('KA', [128, S], BF16) for _ in range(2)]
            VD = [[self.sb('VD', [128, NT, 65], BF16) for _ in range(3)] for _ in range(2)]
            lsem = [[self.dsem() for _ in range(5)] for _ in range(2)]
            for x in range(2):
                for g in range(3):
                    self.op('pool', lambda e: e.memset(VD[x][g].t[:, :, 64:65], 1.0), writes=[VD[x][g]])
            acc = [self.sb('acc', [65, S], F32) for _ in range(2)]
            pts = [self.sb('pt', [128, 512], BF16) for _ in range(4)]
            lr = [self.sb('lr', [128, 512], F32) for _ in range(2)]
            for b_ in lr:
                self.op('pool', lambda e: e.memset(b_.t[:], 0.0), writes=[b_])
            lsc = [self.sb('lsc', [65, 512], F32) for _ in range(2)]
            mkb = self.sb('mkb', [128, 512], BF16)
            for c_ in range(4):
                msrc = self.mk.t[:, 768:896] if c_ % 2 == 0 else self.mk.t[:, 640:768]
                self.op('dve', lambda e: e.tensor_copy(out=mkb.t[:, c_ * 128:(c_ + 1) * 128], in_=msrc), reads=[self.mk], writes=[mkb])
            for x in range(2):
                self.op('pool', lambda e: e.memset(Q3[x].t[64:128, :, :], 0.0), writes=[Q3[x]])
                self.op('pool', lambda e: e.memset(KA[x].t[64:128, :], 0.0), writes=[KA[x]])
            on = [self.sb('on', [64, 512], BF16) for _ in range(2)]
            osem = [self.dsem('pool') for _ in range(2)]
            LA = 2
            nf = 0
            for h in range(16):
                x = h % 2
                self.dma('sp', lsem[x][0], [(Q3[x].t[0:64, g, :], QT[g * 1024 + h * 64:g * 1024 + (h + 1) * 64, :]) for g in range(3)], writes=[Q3[x]])
                self.dma('sp', lsem[x][1], [(KA[x].t[0:64, :], KT[h * 64:(h + 1) * 64, :])], writes=[KA[x]])
                for g, d in enumerate(dils):
                    nb = NT // d
                    pairs = []
                    for r in range(d):
                        vsrc = V[r:S:d, h * 64:(h + 1) * 64].rearrange("(n p) f -> p n f", p=128)
                        pairs.append((VD[x][g].t[:, r * nb:(r + 1) * nb, 0:64], vsrc))
                    self.dma('sp', lsem[x][2 + g], pairs, writes=[VD[x][g]])
                jobs = [(g, d, r, n) for g, d in enumerate(dils) for r in range(d) for n in range(0, NT // d, 2)]
                ac = acc[x]

                def tok(d, r, n0, nblk):
                    return slice(r + d * 128 * n0, r + d * 128 * (n0 + nblk - 1) + d * 127 + 1, d)

                for idx in range(len(jobs) + LA):
                    if idx < len(jobs):
                        g, d, r, n = jobs[idx]
                        ps_ = self.ps[idx % 3]
                        pt_ = pts[idx % 4]
                        for s_ in range(2):
                            nn = n + s_
                            qsl = tok(d, r, nn, 1)
                            if nn > 0:
                                self.mm(ps_, ps_.t[:, s_ * 256:s_ * 256 + 128], KA[x].t[0:128, tok(d, r, nn - 1, 1)], Q3[x].t[0:128, g, qsl], True, True, [KA[x], Q3[x]])
                            self.mm(ps_, ps_.t[:, s_ * 256 + 128:s_ * 256 + 256], KA[x].t[0:128, tok(d, r, nn, 1)], Q3[x].t[0:128, g, qsl], True, True, [KA[x], Q3[x]])
                        lo = 128 if n == 0 else 0
                        self.op('act', lambda e: e.activation(out=pt_.t[:, lo:512], in_=ps_.t[:, lo:512], func=AF.Exp), reads=[ps_], writes=[pt_])
                        self.op('pool', lambda e: e.tensor_tensor(out=pt_.t[:, lo:512], in0=pt_.t[:, lo:512], in1=mkb.t[:, lo:512], op=ALU.mult),
                                reads=[mkb], writes=[pt_])
                    jdx = idx - LA
                    if jdx >= 0:
                        g, d, r, n = jobs[jdx]
                        nb = NT // d
                        po = self.ps[3 + jdx % 2]
                        pt_ = pts[jdx % 4]
                        for s_ in range(2):
                            nn = n + s_
                            if nn > 0:
                                self.mm(po, po.t[0:65, s_ * 128:(s_ + 1) * 128], VD[x][g].t[:, r * nb + nn - 1, 0:65], pt_.t[:, s_ * 256:s_ * 256 + 128], True, False, [VD[x][g], pt_])
                            self.mm(po, po.t[0:65, s_ * 128:(s_ + 1) * 128], VD[x][g].t[:, r * nb + nn, 0:65], pt_.t[:, s_ * 256 + 128:s_ * 256 + 256], nn == 0, True, [VD[x][g], pt_])
                        asl = tok(d, r, n, 2)
                        if g == 0:
                            self.op('act', lambda e: e.activation(out=ac.t[:, asl], in_=po.t[0:65, 0:256], func=AF.Copy), reads=[po], writes=[ac])
                        else:
                            self.op('dve', lambda e: e.tensor_tensor(out=ac.t[:, asl], in0=po.t[0:65, 0:256], in1=ac.t[:, asl], op=ALU.add), reads=[po], writes=[ac])
                        self.pump(1)
                self.flush()
                for cb in range(NB):
                    f = nf % 2
                    nf += 1
                    self.dil_fin(ac, lr[f], on[f], osem[f], self.ps[5 + f], OT, h, cb, lsc[f])
        self.oproj_pass(I['dil_w_out'], 8, hsrc, hdst)

    def dil_fin(self, ac, lr_, on_, osem_, rb, OT, h, cb, sc_):
        csl = slice(cb * 512, (cb + 1) * 512)
        self.defer([
            lambda: self.op('dve', lambda e: e.reciprocal(out=lr_.t[64:65, :], in_=ac.t[64:65, csl]), reads=[ac], writes=[lr_]),
            None, None,
            lambda: self.bcast_row64(rb, lr_),
            None,
            lambda: self.op('dve', lambda e: e.tensor_tensor(out=on_.t[:, :], in0=ac.t[0:64, csl], in1=rb.t[0:64, :], op=ALU.mult), reads=[ac, rb], writes=[on_]),
            lambda: self.dma('pool', osem_, [(OT[h * 64:(h + 1) * 64, csl], on_.t[:, :])], reads=[on_]),
        ])

    def sgu_mixer(self, L, gcol, hsrc, hdst):
        I = self.inp
        with self.phase():
            W = self.sb('sw', [128, 8, 4 * D], BF16)
            Wo = self.sb('swo', [128, 16, D], BF16)
            self.load_w(W, I['sgu_w_in'], 8)
            self.load_w(Wo, I['sgu_w_out'], 16)
            gv = self.sb('gv', [128, 2 * D], F32)
            self.dma('sp', self.dsem(), [(gv.t[:], I['sgu_norm_v'][0:1, :].broadcast_to([128, 2 * D]))], writes=[gv])
            bs = self.sb('bs', [128, 8 * 128], F32)
            self.op('pool', lambda e: e.memset(bs.t[:], 0.0), writes=[bs])
            self.dma('sp', self.dsem(), [(bs.t[0:1, :], I['sgu_b_s'].rearrange("g t -> (g t)").unsqueeze(0))], writes=[bs])
            WmT = self.sb('WmT', [128, 8, 128], BF16)
            st = self.alloc_norm()
            ws = st['hs'][0]
            self.dma('sp', st['hsem'][0], [(ws.t[:, g * 128:(g + 1) * 128], I['sgu_w_s'][g]) for g in range(8)], writes=[ws])
            for g in range(8):
                pt = self.ps[g % 2]
                self.op('pe', lambda e: e.transpose(out=pt.t[:, 0:128], in_=ws.t[:, g * 128:(g + 1) * 128], identity=self.ident.t[:]), reads=[ws, self.ident], writes=[pt])
                self.op('dve', lambda e: e.tensor_tensor(out=WmT.t[:, g, :], in0=pt.t[:, 0:128], in1=self.mk.t[:, 640:768], op=ALU.mult), reads=[pt, self.mk], writes=[WmT])
            uT = [self.sb('uT', [128, 4, 512], BF16) for _ in range(4)]
            vt = self.sb('vt', [128, 2 * D], F32)
            jk2 = self.sb('jk2', [128, 2 * D], BF16)
            vn = [self.sb('vn', [128, 2 * D], BF16) for _ in range(2)]
            zT = [self.sb('zT', [128, 16, 128], BF16) for _ in range(2)]
            ss2 = [self.sb('ss2', [128, 1], F32) for _ in range(2)]
            ms2 = [self.sb('ms2', [128, 1], F32) for _ in range(2)]
            rs2 = [self.sb('rs2', [128, 1], F32) for _ in range(2)]
            hres = [self.sb('hres', [128, D], F32) for _ in range(2)]
            hrsem = [self.dsem() for _ in range(2)]
            hssem = [self.dsem('pool') for _ in range(2)]
            self.norm_prep(st, 0, hsrc)
            hnT = self.norm_trans(st, gcol)
            for b in range(NB):
                if b > 0:
                    self.norm_trans(st, gcol)
                if b + 1 < NB:
                    self.norm_prep(st, b + 1, hsrc)
                for fc in range(16):
                    pu = self.ps[fc % 2]
                    for k in range(8):
                        self.mm(pu, pu.t[:, :], W.t[:, k, fc * 128:(fc + 1) * 128], hnT[k].t[:, :], k == 0, k == 7, [W, hnT[k]])
                    u_ = uT[fc // 4]
                    self.op('act', lambda e: e.activation(out=u_.t[:, fc % 4, :], in_=pu.t[:, :], func=AF.Gelu), reads=[pu], writes=[u_])
                for i in range(4):
                    ti = 4 * b + i
                    f = i % 2
                    hr = hres[f]
                    self.dma('sp', hrsem[f], [(hr.t[:], hsrc[0][ti * 128:(ti + 1) * 128, :])], reads=[hsrc[1][ti]], writes=[hr])
                    for vg in range(4):
                        pv = self.ps[2 + vg % 2]
                        for k in range(8):
                            self.mm(pv, pv.t[:, :], hnT[k].t[:, i * 128:(i + 1) * 128], W.t[:, k, 2 * D + vg * 512:2 * D + (vg + 1) * 512], k == 0, k == 7, [hnT[k], W])
                        self.op('act', lambda e: e.activation(out=vt.t[:, vg * 512:(vg + 1) * 512], in_=pv.t[:, :], func=AF.Gelu), reads=[pv], writes=[vt])
                    self.op('act', lambda e: e.activation(out=jk2.t[:], in_=vt.t[:], func=AF.Square, accum_out=ss2[f].t[:, 0:1]), reads=[vt], writes=[jk2, ss2[f]])
                    self.rstd_of(ss2[f], 2 * D, 1e-6, ms2[f], rs2[f])
                    self.op('dve', lambda e: e.scalar_tensor_tensor(out=vn[f].t[:], in0=vt.t[:], scalar=rs2[f].t[:, 0:1], in1=gv.t[:], op0=ALU.mult, op1=ALU.mult),
                            reads=[vt, rs2[f], gv], writes=[vn[f]])
                    for q4 in range(4):
                        pm = self.ps[4 + q4 % 2]
                        for c in range(4):
                            fc = q4 * 4 + c
                            g = fc // 2
                            self.mm(pm, pm.t[:, c * 128:(c + 1) * 128], vn[f].t[:, fc * 128:(fc + 1) * 128], WmT.t[:, g, :], True, False, [vn[f], WmT])
                            self.mm(pm, pm.t[:, c * 128:(c + 1) * 128], self.ones0.t[:, 0:128], bs.t[:, g * 128:(g + 1) * 128], False, True, [self.ones0, bs])
                        self.op('dve', lambda e: e.tensor_tensor(out=zT[f].t[:, q4 * 4:(q4 + 1) * 4, :], in0=pm.t[:, :].rearrange("p (c t) -> p c t", t=128),
                                                                 in1=uT[q4].t[:, :, i * 128:(i + 1) * 128], op=ALU.mult), reads=[pm, uT[q4]], writes=[zT[f]])
                    for half in range(2):
                        hsl = slice(half * 512, (half + 1) * 512)
                        po = self.ps[6 + half]
                        for fc in range(16):
                            self.mm(po, po.t[:, :], zT[f].t[:, fc, :], Wo.t[:, fc, hsl], fc == 0, fc == 15, [zT[f], Wo])
                        self.op('dve', lambda e: e.tensor_tensor(out=hr.t[:, hsl], in0=po.t[:], in1=hr.t[:, hsl], op=ALU.add), reads=[po, hr], writes=[hr])
                    self.dma('pool', hssem[f], [(hdst[0][ti * 128:(ti + 1) * 128, :], hr.t[:])], reads=[hr], writes=[hdst[1][ti]])


INPUT_SHAPES = {
    'x': [S, D], 'p': [4, S, 256],
    'w_ffn1_in': [4, D, 2 * DFF], 'w_ffn1_out': [4, DFF, D],
    'w_ffn2_in': [4, D, 2 * DFF], 'w_ffn2_out': [4, DFF, D],
    'w_ple_gate': [4, D, D], 'b_ple_gate': [4, D], 'w_ple_proj': [4, 256, D],
    'fox_w_in': [D, 3 * D + 16], 'fox_b_f': [1, 16], 'fox_w_out': [D, D],
    'dil_w_in': [D, 5 * D], 'dil_w_out': [D, D],
    'diff_w_in': [D, 3 * D], 'diff_lambda': [1, 256], 'diff_subln': [128, 1], 'diff_w_out': [D, D],
    'sgu_w_in': [D, 4 * D], 'sgu_norm_v': [1, 2 * D], 'sgu_w_s': [8, 128, 128], 'sgu_b_s': [8, 128], 'sgu_w_out': [2 * D, D],
    'norm_final': [1, D],
    'gam': [128, 128], 'ident': [128, 128], 'cs': [128, NT * 16], 'masks': [128, 1024],
}


def build(layers=(0, 1, 2, 3), mixers=True, ffn=True, ple=True, dbg=None):
    nc = bass.Bass("TRN2", target_bir_lowering=False)
    class LazyIn(dict):
        def __missing__(self, n):
            self[n] = nc.dram_tensor(n, INPUT_SHAPES[n], F32, kind="ExternalInput").ap()
            return self[n]
    I = LazyIn()
    out_d = nc.dram_tensor("out", [S, D], F32, kind="ExternalOutput").ap()
    hbuf = nc.dram_tensor("hbuf", [S, D], F32, kind="Internal").ap()
    with ExitStack() as es:
        k = K(nc, es)
        k.inp = I
        k.scr = {}
        for n, shp, dt in [('QT', [3 * D, S], BF16), ('KT', [D, S], BF16), ('V', [S, D], BF16), ('OT', [2 * D, S], BF16),
                           ('C3', [48, S], BF16)]:
            k.scr[n] = (nc.dram_tensor('scr_' + n, shp, dt, kind="Internal").ap(), Buf(None))
        k.outtok = Buf(None)
        xs = (I['x'], [Buf(None) for _ in range(NT)])
        hb = (hbuf, [Buf(None) for _ in range(NT)])
        k.ident = k.sb('ident', [128, 128], F32)
        k.gam = k.sb('gam', [128, 128], F32)
        k.neghalf = k.sb('neghalf', [128, 512], F32)
        k.ones32 = k.sb('ones32', [128, 128], F32)
        k.dma('sp', k.dsem(), [(k.ident.t[:], I['ident'][:, :])], writes=[k.ident])
        k.dma('sp', k.dsem(), [(k.gam.t[:], I['gam'][:, :])], writes=[k.gam])
        k.cs = k.sb('cs', [128, NT * 16], F32)
        k.mk = k.sb('mk', [128, 1024], F32)
        k.dma('sp', k.dsem(), [(k.cs.t[:], I['cs'][:, :])], writes=[k.cs])
        k.dma('sp', k.dsem(), [(k.mk.t[:], I['masks'][:, :])], writes=[k.mk])
        k.op('pool', lambda e: e.memset(k.neghalf.t[:], -0.5), writes=[k.neghalf])
        k.op('pool', lambda e: e.memset(k.ones32.t[:], 1.0), writes=[k.ones32])
        k.sel64 = k.sb('sel64', [128, 65], F32)
        k.onesT = k.sb('onesT', [128, 65], F32)
        k.ones0 = k.sb('ones0', [128, 128], F32)
        for cb_, rows in [(k.sel64, (64, 65)), (k.onesT, (0, 64)), (k.ones0, (0, 1))]:
            k.op('pool', lambda e: e.memset(cb_.t[:], 0.0), writes=[cb_])
            k.op('pool', lambda e: e.memset(cb_.t[rows[0]:rows[1], :], 1.0), writes=[cb_])
        src = xs
        for L in layers:
            if ffn:
                k.ffn_pass(I['w_ffn1_in'][L], I['w_ffn1_out'][L], (0 * 4 + L) * 8, src, hb)
                src = hb
            if mixers:
                getattr(k, ['fox_mixer', 'dil_mixer', 'diff_mixer', 'sgu_mixer'][L])(L, (1 * 4 + L) * 8, src, hb)
                src = hb
            if ffn:
                k.ffn_pass(I['w_ffn2_in'][L], I['w_ffn2_out'][L], (2 * 4 + L) * 8, src, hb)
                src = hb
            if ple:
                last = (L == layers[-1])
                k.ple_pass(L, (3 * 4 + L) * 8, src, hb, final_out=out_d if last else None)
                src = hb
        if not (ple and len(layers)):
            k.final_pass(src, out_d)
        k.barrier()
    nc.used_inputs = list(I.keys())
    return nc


def host_consts():
    c = {}
    c['ident'] = np.eye(128, dtype=np.float32)
    half = 8
    inv_freq = (500000.0 ** (-np.arange(0, 16, 2, dtype=np.float32) / 16)).astype(np.float32)
    ang = np.arange(S, dtype=np.float32)[:, None] * inv_freq[None, :]
    cs = np.concatenate([np.cos(ang), np.sin(ang)], axis=1).astype(np.float32)
    c['cs'] = np.ascontiguousarray(cs.reshape(NT, 128, 16).transpose(1, 0, 2).reshape(128, NT * 16))
    kk = np.arange(128)[:, None]
    qq = np.arange(128)[None, :]
    cur = np.where(kk <= qq, 0.0, NEG).astype(np.float32)
    prev = np.where(kk >= qq, 0.0, NEG).astype(np.float32)
    m = np.zeros((128, 1024), np.float32)
    m[:, 0:128] = cur
    m[:, 128:256] = prev
    m[:, 256:384] = cur
    m[:, 384:512] = prev
    m[:, 512:640] = cur
    m[:, 640:768] = (kk <= qq).astype(np.float32)
    m[:, 768:896] = (kk >= qq).astype(np.float32)
    c['masks'] = m
    return c


def make_in_maps(inputs):
    f = lambda a: np.ascontiguousarray(np.asarray(a, dtype=np.float32))
    shared = {}
    for n in ['w_ffn1_in', 'w_ffn1_out', 'w_ffn2_in', 'w_ffn2_out', 'w_ple_gate', 'b_ple_gate', 'w_ple_proj']:
        shared[n] = f(inputs[n])
    for n in ['fox_w_in', 'fox_b_f', 'fox_w_out', 'dil_w_in', 'dil_w_out', 'diff_w_in', 'diff_w_out', 'sgu_w_in',
              'sgu_norm_v', 'sgu_w_s', 'sgu_b_s', 'sgu_w_out']:
        shared[n] = f(np.asarray(inputs[n])[0])
    shared['fox_b_f'] = shared['fox_b_f'].reshape(1, 16)
    shared['sgu_norm_v'] = shared['sgu_norm_v'].reshape(1, 2 * D)
    shared['diff_lambda'] = f(np.asarray(inputs['diff_lambda'])[0]).reshape(1, 256)
    shared['diff_subln'] = f(np.asarray(inputs['diff_subln'])[0]).reshape(128, 1)
    shared['norm_final'] = f(inputs['norm_final']).reshape(1, D)
    g = np.stack([np.asarray(inputs[n], dtype=np.float32) for n in ['norm_ffn1', 'norm_mix', 'norm_ffn2', 'norm_ple']])
    shared['gam'] = np.ascontiguousarray(g.reshape(16, 8, 128).transpose(2, 0, 1).reshape(128, 128))
    shared.update(host_consts())
    x = np.asarray(inputs['x'], dtype=np.float32)
    p = np.asarray(inputs['p'], dtype=np.float32)
    maps = []
    for b in range(8):
        m = dict(shared)
        m['x'] = np.ascontiguousarray(x[b])
        m['p'] = np.ascontiguousarray(p[:, b])
        maps.append(m)
    return maps


def kernel(**inputs):
    nc = build()
    maps = make_in_maps(inputs)
    maps = [{n: m[n] for n in nc.used_inputs} for m in maps]
    res = run_bass_kernel_spmd(nc, maps, core_ids=list(range(8)))
    return np.stack([np.asarray(r['out'], dtype=np.float32) for r in res.results], axis=0)
```

```python
import math
from contextlib import ExitStack, contextmanager
import numpy as np
import ml_dtypes
import concourse.bass as bass
import concourse.mybir as mybir
from concourse.bass_utils import run_bass_kernel_spmd

F32 = mybir.dt.float32
BF16 = mybir.dt.bfloat16
AF = mybir.ActivationFunctionType
ALU = mybir.AluOpType
AX = mybir.AxisListType

S = 4096
D = 1024
NT = 32
NB = 8
DFF = 2816
NEG = -30000.0
SKIP = set()


class Buf:
    def __init__(self, t):
        self.t = t
        self.w = None
        self.r = {}


class DSem:
    def __init__(self, sem):
        self.sem = sem
        self.cum = 0


class K:
    def __init__(self, nc, es):
        self.nc = nc
        self.es = es
        self.eng = {'pe': nc.tensor, 'act': nc.scalar, 'dve': nc.vector, 'pool': nc.gpsimd, 'sp': nc.sync}
        self.sem = {e: es.enter_context(nc.semaphore('s_' + e)) for e in self.eng}
        self.cnt = {e: 0 for e in self.eng}
        self.known = {e: {} for e in self.eng}
        self.dsems = [DSem(es.enter_context(nc.semaphore('d%d' % i))) for i in range(40)]
        self.dsems_sw = [DSem(es.enter_context(nc.semaphore('w%d' % i))) for i in range(24)]
        for d in self.dsems:
            d.kind = 'sp'
        for d in self.dsems_sw:
            d.kind = 'pool'
        self.dsi = 0
        self.dsi_sw = 0
        self.cur = es
        self.ps = [Buf(es.enter_context(nc.psum_tensor('ps%d' % i, [128, 512], F32))) for i in range(8)]
        self.uid = 0
        self.dq = []

    def defer(self, fns):
        self.dq.extend(fns)

    def pump(self, n=1):
        for _ in range(min(n, len(self.dq))):
            f = self.dq.pop(0)
            if f is not None:
                f()

    def flush(self, keep=0):
        while len(self.dq) > keep:
            f = self.dq.pop(0)
            if f is not None:
                f()

    def dsem(self, kind='sp'):
        if kind == 'pool':
            d = self.dsems_sw[self.dsi_sw % len(self.dsems_sw)]
            self.dsi_sw += 1
            return d
        d = self.dsems[self.dsi % len(self.dsems)]
        self.dsi += 1
        return d

    def _wait(self, e, deps):
        kn = self.known[e]
        for (sem, val) in deps:
            key = id(sem)
            if kn.get(key, 0) >= val:
                continue
            self.eng[e].wait_ge(sem, val)
            kn[key] = val

    def _deps(self, reads, writes):
        deps = []
        for t in reads:
            if t.w:
                deps.append(t.w)
        for t in writes:
            if t.w:
                deps.append(t.w)
            deps.extend(t.r.values())
        return deps

    def op(self, e, fn, reads=(), writes=()):
        deps = self._deps(reads, writes)
        own = self.sem[e]
        if e == 'pe':
            deps = [d for d in deps if d[0] is not own]
        self._wait(e, deps)
        inst = fn(self.eng[e])
        self.cnt[e] += 1
        inst.then_inc(own, 1)
        tok = (own, self.cnt[e])
        for t in reads:
            t.r[id(own)] = tok
        for t in writes:
            t.w = tok
            t.r = {}

    def dma(self, q, ds, pairs, reads=(), writes=()):
        assert ds.kind == q, (ds.kind, q)
        deps = self._deps(reads, writes)
        if ds.cum:
            deps.append((ds.sem, ds.cum))
        self._wait(q, deps)
        for (o, i) in pairs:
            self.eng[q].dma_start(out=o, in_=i).then_inc(ds.sem, 16)
            ds.cum += 16
        tok = (ds.sem, ds.cum)
        for t in reads:
            t.r[id(ds.sem)] = tok
        for t in writes:
            t.w = tok
            t.r = {}

    def barrier(self):
        deps = [(self.sem[e], self.cnt[e]) for e in self.eng if self.cnt[e]]
        deps += [(d.sem, d.cum) for d in self.dsems + self.dsems_sw if d.cum]
        for e in self.eng:
            self._wait(e, deps)

    @contextmanager
    def phase(self):
        prev = self.cur
        with ExitStack() as es:
            self.cur = es
            yield
            self.flush()
            self.barrier()
        self.cur = prev
        for b in self.ps:
            b.w = None
            b.r = {}

    def sb(self, name, shape, dt):
        self.uid += 1
        return Buf(self.cur.enter_context(self.nc.sbuf_tensor('%s_%d' % (name, self.uid), shape, dt)))

    def mm(self, pb, out, lhsT, rhs, start, stop, reads):
        self.op('pe', lambda e: e.matmul(out, lhsT, rhs, start=start, stop=stop), reads=reads, writes=[pb])

    def rstd_of(self, ssb, n_feat, eps, ms, rstd):
        self.op('pool', lambda e: e.tensor_scalar(out=ms.t[:], in0=ssb.t[:], scalar1=1.0 / n_feat, scalar2=eps,
                                                  op0=ALU.mult, op1=ALU.add), reads=[ssb], writes=[ms])
        np_, nf = ms.t.shape[0], ms.t.shape[1]
        self.op('pool', lambda e: e.tensor_tensor(out=rstd.t[:], in0=ms.t[:], in1=self.neghalf.t[0:np_, 0:nf],
                                                  op=ALU.pow), reads=[ms, self.neghalf], writes=[rstd])

    def alloc_norm(self, nh=1):
        st = {}
        st['hs'] = [self.sb('hs', [128, D], F32) for _ in range(4)]
        st['hsem'] = [self.dsem() for _ in range(4)]
        st['junk'] = [self.sb('junk', [128, D], BF16) for _ in range(2)]
        st['ss'] = [self.sb('ss', [128, 1], F32) for _ in range(4)]
        st['ms'] = [self.sb('ms', [128, 1], F32) for _ in range(4)]
        st['rstd'] = [self.sb('rstd', [128, 1], F32) for _ in range(4)]
        st['hnT'] = [[self.sb('hnT', [128, 512], BF16) for _ in range(8)] for _ in range(nh)]
        return st

    def norm_prep(self, st, b, hsrc):
        for i in range(4):
            ti = 4 * b + i
            hs = st['hs'][i]
            self.dma('sp', st['hsem'][i], [(hs.t[:], hsrc[0][ti * 128:(ti + 1) * 128, :])], reads=[hsrc[1][ti]], writes=[hs])
            jk = st['junk'][i % 2]
            ss, ms, rstd = st['ss'][i], st['ms'][i], st['rstd'][i]
            self.op('act', lambda e: e.activation(out=jk.t[:], in_=hs.t[:], func=AF.Square, accum_out=ss.t[:, 0:1]),
                    reads=[hs], writes=[jk, ss])
            self.rstd_of(ss, D, 1e-6, ms, rstd)
            self.op('dve', lambda e: e.tensor_scalar(out=hs.t[:], in0=hs.t[:], scalar1=rstd.t[:, 0:1], scalar2=None,
                                                     op0=ALU.mult), reads=[hs, rstd], writes=[hs])

    def norm_trans(self, st, gcol, hset=0, ps_ids=(6, 7)):
        for c in range(8):
            pt = self.ps[ps_ids[c % len(ps_ids)]]
            for i in range(4):
                hs = st['hs'][i]
                self.op('pe', lambda e: e.transpose(out=pt.t[:, i * 128:(i + 1) * 128], in_=hs.t[:, c * 128:(c + 1) * 128],
                                                    identity=self.ident.t[:]), reads=[hs, self.ident], writes=[pt])
            hn = st['hnT'][hset][c]
            g = self.gam.t[:, gcol + c:gcol + c + 1]
            if c % 2 == 0:
                self.op('act', lambda e: e.activation(out=hn.t[:], in_=pt.t[:], func=AF.Copy, scale=g),
                        reads=[pt, self.gam], writes=[hn])
            else:
                self.op('dve', lambda e: e.tensor_scalar(out=hn.t[:], in0=pt.t[:], scalar1=g, scalar2=None, op0=ALU.mult),
                        reads=[pt, self.gam], writes=[hn])
        return st['hnT'][hset]

    def load_w(self, W, wd, nk, parts=1):
        N = wd.shape[1]
        step = (N + parts - 1) // parts
        pairs = []
        for k in range(nk):
            for c0 in range(0, N, step):
                c1 = min(N, c0 + step)
                pairs.append((W.t[:, k, c0:c1], wd[k * 128:(k + 1) * 128, c0:c1]))
        self.dma('pool', self.dsem('pool'), pairs, writes=[W])

    def ffn_pass(self, w_in_d, w_out_d, gcol, hsrc, hdst):
        with self.phase():
            Win = self.sb('win', [128, 8, 2 * DFF], BF16)
            Wout = self.sb('wout', [128, 22, D], BF16)
            HF = 11 * 128
            WinC = [Buf(Win.t), Buf(Win.t)]
            WoutC = [Buf(Wout.t), Buf(Wout.t)]
            st = self.alloc_norm()
            for c_ in range(2):
                prs = []
                for k in range(8):
                    for base in (0, DFF):
                        prs.append((Win.t[:, k, base + c_ * HF:base + (c_ + 1) * HF], w_in_d[k * 128:(k + 1) * 128, base + c_ * HF:base + (c_ + 1) * HF]))
                self.dma('pool', self.dsem('pool'), prs, writes=[WinC[c_]])
                if c_ == 0:
                    self.norm_prep(st, 0, hsrc)
            for c_ in range(2):
                prs = [(Wout.t[:, j, :], w_out_d[j * 128:(j + 1) * 128, :]) for j in range(c_ * 11, (c_ + 1) * 11)]
                self.dma('pool', self.dsem('pool'), prs, writes=[WoutC[c_]])
            actT = [self.sb('actT', [128, 512], BF16) for _ in range(22)]
            sg = [self.sb('sg', [128, 512], F32) for _ in range(2)]
            hres = [self.sb('hres', [128, D], F32) for _ in range(2)]
            hrsem = [self.dsem() for _ in range(2)]
            hssem = [self.dsem('pool') for _ in range(2)]
            hnT = self.norm_trans(st, gcol)
            for b in range(NB):
                if b + 1 < NB:
                    self.norm_prep(st, b + 1, hsrc)
                for j in range(22):
                    pg, pu = self.ps[(2 * j) % 4], self.ps[(2 * j + 1) % 4]
                    for k in range(8):
                        self.mm(pg, pg.t[:, :], Win.t[:, k, j * 128:(j + 1) * 128], hnT[k].t[:, :], k == 0, k == 7, [WinC[j // 11], hnT[k]])
                    for k in range(8):
                        self.mm(pu, pu.t[:, :], Win.t[:, k, DFF + j * 128:DFF + (j + 1) * 128], hnT[k].t[:, :], k == 0, k == 7, [WinC[j // 11], hnT[k]])
                    s_ = sg[j % 2]
                    a_ = actT[j]
                    self.op('act', lambda e: e.activation(out=s_.t[:], in_=pg.t[:], func=AF.Silu), reads=[pg], writes=[s_])
                    self.op('dve', lambda e: e.tensor_tensor(out=a_.t[:], in0=pu.t[:], in1=s_.t[:], op=ALU.mult),
                            reads=[pu, s_], writes=[a_])
                if b + 1 < NB:
                    self.norm_trans(st, gcol)
                for i in range(4):
                    ti = 4 * b + i
                    hr = hres[i % 2]
                    self.dma('sp', hrsem[i % 2], [(hr.t[:], hsrc[0][ti * 128:(ti + 1) * 128, :])], reads=[hsrc[1][ti]], writes=[hr])
                    for half in range(2):
                        po = self.ps[4 + half]
                        for j in range(22):
                            self.mm(po, po.t[:, :], actT[j].t[:, i * 128:(i + 1) * 128], Wout.t[:, j, half * 512:(half + 1) * 512],
                                    j == 0, j == 21, [actT[j], WoutC[j // 11]])
                        self.op('dve', lambda e: e.scalar_tensor_tensor(out=hr.t[:, half * 512:(half + 1) * 512], in0=po.t[:], scalar=0.5,
                                                                        in1=hr.t[:, half * 512:(half + 1) * 512], op0=ALU.mult, op1=ALU.add),
                                reads=[po, hr], writes=[hr])
                    self.dma('pool', hssem[i % 2], [(hdst[0][ti * 128:(ti + 1) * 128, :], hr.t[:])], reads=[hr], writes=[hdst[1][ti]])

    def ple_pass(self, L, gcol, hsrc, hdst, final_out=None):
        I = self.inp
        with self.phase():
            Wg = self.sb('wg', [128, 8, D], BF16)
            Wp = self.sb('wp', [128, 2, D], BF16)
            self.load_w(Wg, I['w_ple_gate'][L], 8)
            self.load_w(Wp, I['w_ple_proj'][L], 2)
            bg = self.sb('bg', [128, D], F32)
            self.op('pool', lambda e: e.memset(bg.t[:], 0.0), writes=[bg])
            self.dma('sp', self.dsem(), [(bg.t[0:1, :], I['b_ple_gate'][L:L + 1, :])], writes=[bg])
            st = self.alloc_norm(2)
            pl = [self.sb('pl', [128, 256], F32) for _ in range(2)]
            plsem = [self.dsem() for _ in range(2)]
            pT = [self.sb('pT', [128, 256], BF16) for _ in range(2)]
            gate = [self.sb('gate', [128, 512], F32) for _ in range(2)]
            hres = [self.sb('hres', [128, D], F32) for _ in range(4)]
            hrsem = [self.dsem() for _ in range(4)]
            hssem = [self.dsem('pool') for _ in range(4)]
            if final_out is not None:
                gf = self.sb('gf', [128, D], F32)
                self.dma('sp', self.dsem(), [(gf.t[:], I['norm_final'][0:1, :].broadcast_to([128, D]))], writes=[gf])
                fjk = [self.sb('fjk', [128, D], BF16) for _ in range(2)]
                fss = [self.sb('fss', [128, 1], F32) for _ in range(2)]
                fms = [self.sb('fms', [128, 1], F32) for _ in range(2)]
                frs = [self.sb('frs', [128, 1], F32) for _ in range(2)]
            n = 0
            self.norm_prep(st, 0, hsrc)
            self.norm_trans(st, gcol, 0)
            for b in range(NB):
                hnT = st['hnT'][b % 2]
                if b + 1 < NB:
                    self.norm_prep(st, b + 1, hsrc)
                for i in range(4):
                    if i == 1 and b + 1 < NB:
                        self.norm_trans(st, gcol, (b + 1) % 2)
                    ti = 4 * b + i
                    p_, pt_, hr = pl[i % 2], pT[i % 2], hres[i % 4]
                    self.dma('sp', plsem[i % 2], [(p_.t[:], I['p'][L, ti * 128:(ti + 1) * 128, :])], writes=[p_])
                    self.dma('sp', hrsem[i % 4], [(hr.t[:], hsrc[0][ti * 128:(ti + 1) * 128, :])], reads=[hsrc[1][ti]], writes=[hr])
                    pp = self.ps[4]
                    for kc in range(2):
                        self.op('pe', lambda e: e.transpose(out=pp.t[:, kc * 128:(kc + 1) * 128], in_=p_.t[:, kc * 128:(kc + 1) * 128],
                                                            identity=self.ident.t[:]), reads=[p_, self.ident], writes=[pp])
                    self.op('act', lambda e: e.activation(out=pt_.t[:], in_=pp.t[:, 0:256], func=AF.Copy), reads=[pp], writes=[pt_])
                    for half in range(2):
                        hsl = slice(half * 512, (half + 1) * 512)
                        pg = self.ps[n % 2]
                        pq = self.ps[2 + n % 2]
                        g_ = gate[n % 2]
                        n += 1
                        for k in range(8):
                            self.mm(pg, pg.t[:, :], hnT[k].t[:, i * 128:(i + 1) * 128], Wg.t[:, k, hsl], k == 0, False, [hnT[k], Wg])
                        self.mm(pg, pg.t[:, :], self.ones0.t[:, 0:128], bg.t[:, hsl], False, True, [self.ones0, bg])
                        for kc in range(2):
                            self.mm(pq, pq.t[:, :], pt_.t[:, kc * 128:(kc + 1) * 128], Wp.t[:, kc, hsl], kc == 0, kc == 1, [pt_, Wp])
                        self.op('act', lambda e: e.activation(out=g_.t[:], in_=pg.t[:], func=AF.Sigmoid), reads=[pg], writes=[g_])
                        self.op('dve', lambda e: e.tensor_tensor(out=g_.t[:], in0=pq.t[:], in1=g_.t[:], op=ALU.mult), reads=[pq, g_], writes=[g_])
                        self.op('dve', lambda e: e.tensor_tensor(out=hr.t[:, hsl], in0=g_.t[:], in1=hr.t[:, hsl], op=ALU.add), reads=[g_, hr], writes=[hr])
                    if final_out is None:
                        self.dma('pool', hssem[i % 4], [(hdst[0][ti * 128:(ti + 1) * 128, :], hr.t[:])], reads=[hr], writes=[hdst[1][ti]])
                    else:
                        j_, s_, m_, r_ = fjk[i % 2], fss[i % 2], fms[i % 2], frs[i % 2]
                        self.op('act', lambda e: e.activation(out=j_.t[:], in_=hr.t[:], func=AF.Square, accum_out=s_.t[:, 0:1]), reads=[hr], writes=[j_, s_])
                        self.rstd_of(s_, D, 1e-6, m_, r_)
                        self.op('dve', lambda e: e.scalar_tensor_tensor(out=hr.t[:], in0=hr.t[:], scalar=r_.t[:, 0:1], in1=gf.t[:], op0=ALU.mult, op1=ALU.mult),
                                reads=[hr, r_, gf], writes=[hr])
                        self.dma('pool', hssem[i % 4], [(final_out[ti * 128:(ti + 1) * 128, :], hr.t[:])], reads=[hr], writes=[self.outtok])

    def final_pass(self, hsrc, out_d):
        I = self.inp
        with self.phase():
            gf = self.sb('gf', [128, D], F32)
            self.dma('sp', self.dsem(), [(gf.t[:], I['norm_final'][0:1, :].broadcast_to([128, D]))], writes=[gf])
            hs = [self.sb('fh', [128, D], F32) for _ in range(3)]
            hsem = [self.dsem() for _ in range(3)]
            fssem = [self.dsem('pool') for _ in range(3)]
            jk = [self.sb('fj', [128, D], BF16) for _ in range(2)]
            ss = [self.sb('fss', [128, 1], F32) for _ in range(3)]
            ms = [self.sb('fms', [128, 1], F32) for _ in range(3)]
            rs = [self.sb('frs', [128, 1], F32) for _ in range(3)]
            for ti in range(NT):
                h_, s_, m_, r_, j_ = hs[ti % 3], ss[ti % 3], ms[ti % 3], rs[ti % 3], jk[ti % 2]
                self.dma('sp', hsem[ti % 3], [(h_.t[:], hsrc[0][ti * 128:(ti + 1) * 128, :])], reads=[hsrc[1][ti]], writes=[h_])
                self.op('act', lambda e: e.activation(out=j_.t[:], in_=h_.t[:], func=AF.Square, accum_out=s_.t[:, 0:1]), reads=[h_], writes=[j_, s_])
                self.rstd_of(s_, D, 1e-6, m_, r_)
                self.op('dve', lambda e: e.scalar_tensor_tensor(out=h_.t[:], in0=h_.t[:], scalar=r_.t[:, 0:1], in1=gf.t[:], op0=ALU.mult, op1=ALU.mult),
                        reads=[h_, r_, gf], writes=[h_])
                self.dma('pool', fssem[ti % 3], [(out_d[ti * 128:(ti + 1) * 128, :], h_.t[:])], reads=[h_], writes=[self.outtok])


    def rope(self, q, ti, tmp):
        v = q.t[:, :].rearrange("p (h d) -> p h d", d=64)
        x1, x2 = v[:, :, 0:8], v[:, :, 8:16]
        cos = self.cs.t[:, ti * 16:ti * 16 + 8].unsqueeze(1).broadcast_to([128, 8, 8])
        sin = self.cs.t[:, ti * 16 + 8:ti * 16 + 16].unsqueeze(1).broadcast_to([128, 8, 8])
        t = [b.t[:, :].rearrange("p (h d) -> p h d", d=8) for b in tmp]
        self.op('dve', lambda e: e.tensor_tensor(out=t[0], in0=x1, in1=cos, op=ALU.mult), reads=[q, self.cs], writes=[tmp[0]])
        self.op('dve', lambda e: e.tensor_tensor(out=t[1], in0=x2, in1=sin, op=ALU.mult), reads=[q, self.cs], writes=[tmp[1]])
        self.op('pool', lambda e: e.tensor_tensor(out=t[2], in0=x1, in1=sin, op=ALU.mult), reads=[q, self.cs], writes=[tmp[2]])
        self.op('pool', lambda e: e.tensor_tensor(out=t[3], in0=x2, in1=cos, op=ALU.mult), reads=[q, self.cs], writes=[tmp[3]])
        self.op('dve', lambda e: e.tensor_tensor(out=x1, in0=t[0], in1=t[1], op=ALU.subtract), reads=[tmp[0], tmp[1], tmp[2], tmp[3]], writes=[q])
        self.op('pool', lambda e: e.tensor_tensor(out=x2, in0=t[2], in1=t[3], op=ALU.add), reads=[tmp[2], tmp[3]], writes=[q])

    def proj_pass(self, wd, NF, gcol, hsrc, groups, fox=False, sgu=None):
        if 'proj' in SKIP:
            return
        I = self.inp
        with self.phase():
            W = self.sb('pw', [128, 8, NF], BF16)
            self.load_w(W, wd, 8)
            st = self.alloc_norm(2)
            qs = [self.sb('qs', [128, 512], F32) for _ in range(8)]
            tmp = [[self.sb('rt', [128, 64], F32) for _ in range(4)] for _ in range(2)]
            qst = [self.sb('qst', [128, 512], BF16) for _ in range(2)]
            qsem = [self.dsem() for _ in range(2)]
            vst = [self.sb('vst', [128, 512], BF16) for _ in range(2)]
            vsem = [self.dsem() for _ in range(2)]
            if fox:
                bfb = self.sb('bfb', [128, 16], F32)
                self.dma('sp', self.dsem(), [(bfb.t[:], I['fox_b_f'][0:1, :].broadcast_to([128, 16]))], writes=[bfb])
                LF = self.sb('LF', [128, 512], F32)
                xg = [self.sb('xg', [128, 16], F32) for _ in range(2)]
            nq = 0
            nv = 0
            pending = None
            self.norm_prep(st, 0, hsrc)
            self.norm_trans(st, gcol, 0)
            for b in range(NB):
                hnT = st['hnT'][b % 2]
                if b + 1 < NB:
                    self.norm_prep(st, b + 1, hsrc)
                for gi, (kind, c0, dst, d0, scale, rp) in enumerate(groups):
                    if gi == 1 and b + 1 < NB:
                        self.norm_trans(st, gcol, (b + 1) % 2)
                    for i in range(4):
                        ti = 4 * b + i
                        pm = self.ps[i]
                        for k in range(8):
                            self.mm(pm, pm.t[:, :], hnT[k].t[:, i * 128:(i + 1) * 128], W.t[:, k, c0:c0 + 512], k == 0, k == 7, [hnT[k], W])
                        if kind == 'N':
                            v_ = vst[nv % 2]
                            self.op('act', lambda e: e.activation(out=v_.t[:], in_=pm.t[:], func=AF.Copy), reads=[pm], writes=[v_])
                            self.dma('sp', vsem[nv % 2], [(dst[ti * 128:(ti + 1) * 128, d0:d0 + 512], v_.t[:])], reads=[v_])
                            nv += 1
                        else:
                            q_ = qs[(gi % 2) * 4 + i]
                            self.op('act', lambda e: e.activation(out=q_.t[:], in_=pm.t[:], func=AF.Copy), reads=[pm], writes=[q_])
                            if rp:
                                self.rope(q_, ti, tmp[i % 2])
                    if pending is not None:
                        pending()
                        pending = None
                    if kind == 'T':
                        def _tr(gi=gi, dst=dst, d0=d0, scale=scale, b=b):
                            nonlocal nq
                            for fc in range(4):
                                pt = self.ps[4 + fc % 2]
                                for i in range(4):
                                    q_ = qs[(gi % 2) * 4 + i]
                                    self.op('pe', lambda e: e.transpose(out=pt.t[:, i * 128:(i + 1) * 128], in_=q_.t[:, fc * 128:(fc + 1) * 128],
                                                                        identity=self.ident.t[:]), reads=[q_, self.ident], writes=[pt])
                                s_ = qst[nq % 2]
                                self.op('dve', lambda e: e.tensor_scalar(out=s_.t[:], in0=pt.t[:], scalar1=float(scale), scalar2=None, op0=ALU.mult),
                                        reads=[pt], writes=[s_])
                                self.dma('sp', qsem[nq % 2], [(dst[d0 + fc * 128:d0 + (fc + 1) * 128, b * 512:(b + 1) * 512], s_.t[:])], reads=[s_])
                                nq += 1
                        pending = _tr
                if pending is not None:
                    pending()
                    pending = None
                if fox:
                    for i in range(4):
                        ti = 4 * b + i
                        pm = self.ps[i]
                        x_ = xg[i % 2]
                        for k in range(8):
                            self.mm(pm, pm.t[:, 0:16], hnT[k].t[:, i * 128:(i + 1) * 128], W.t[:, k, 3072:3088], k == 0, k == 7, [hnT[k], W])
                        self.op('dve', lambda e: e.tensor_tensor(out=x_.t[:], in0=pm.t[:, 0:16], in1=bfb.t[:], op=ALU.add), reads=[pm, bfb], writes=[x_])
                        self.op('act', lambda e: e.activation(out=x_.t[:], in_=x_.t[:], func=AF.Exp, scale=-1.0), reads=[x_], writes=[x_])
                        self.op('act', lambda e: e.activation(out=LF.t[:, ti * 16:(ti + 1) * 16], in_=x_.t[:], func=AF.Ln, bias=self.ones32.t[:, 0:1]),
                                reads=[x_, self.ones32], writes=[LF])
            if fox:
                self.fox_cumsum(LF)

    def fox_cumsum(self, LF):
        C3d = self.scr['C3'][0]
        w1, tot = self.ps[0], self.ps[1]
        self.mm(w1, w1.t[:, :], self.mk.t[:, 640:768], LF.t[:, :], True, True, [self.mk, LF])
        self.mm(tot, tot.t[:, :], self.ones32.t[:, :], LF.t[:, :], True, True, [self.ones32, LF])
        tots = self.sb('tots', [128, 512], F32)
        pre = self.sb('pre', [128, 512], F32)
        self.op('act', lambda e: e.activation(out=tots.t[:], in_=tot.t[:], func=AF.Copy), reads=[tot], writes=[tots])
        self.op('dve', lambda e: e.memset(pre.t[:, 0:16], 0.0), writes=[pre])
        for i in range(1, NT):
            self.op('dve', lambda e: e.tensor_tensor(out=pre.t[:, i * 16:(i + 1) * 16], in0=pre.t[:, (i - 1) * 16:i * 16],
                                                     in1=tots.t[:, (i - 1) * 16:i * 16], op=ALU.add), reads=[pre, tots], writes=[pre])
        cn = self.cneg
        self.op('dve', lambda e: e.tensor_tensor(out=cn.t[:], in0=w1.t[:], in1=pre.t[:], op=ALU.add), reads=[w1, pre], writes=[cn])
        c3 = self.sb('c3', [128, NT, 48], F32)
        pb = self.sb('pb', [128, 512], BF16)
        r1 = self.sb('r1', [128, 512], F32)
        r2 = self.sb('r2', [128, 512], F32)
        v3 = lambda a: a.rearrange("p (i h) -> p i h", h=16)
        self.op('dve', lambda e: e.tensor_scalar(out=pb.t[:], in0=cn.t[:], scalar1=-1.0, scalar2=None, op0=ALU.mult), reads=[cn], writes=[pb])
        self.op('dve', lambda e: e.tensor_copy(out=c3.t[:, :, 0:16], in_=v3(pb.t[:, :])), reads=[pb], writes=[c3])
        self.op('dve', lambda e: e.scalar_tensor_tensor(out=r1.t[:], in0=cn.t[:], scalar=-1.0, in1=pb.t[:], op0=ALU.mult, op1=ALU.subtract),
                reads=[cn, pb], writes=[r1])
        self.op('dve', lambda e: e.tensor_copy(out=pb.t[:], in_=r1.t[:]), reads=[r1, c3], writes=[pb])
        self.op('dve', lambda e: e.tensor_copy(out=c3.t[:, :, 16:32], in_=v3(pb.t[:, :])), reads=[pb], writes=[c3])
        self.op('dve', lambda e: e.tensor_tensor(out=r2.t[:], in0=r1.t[:], in1=pb.t[:], op=ALU.subtract), reads=[r1, pb], writes=[r2])
        self.op('dve', lambda e: e.tensor_copy(out=pb.t[:], in_=r2.t[:]), reads=[r2, c3], writes=[pb])
        self.op('dve', lambda e: e.tensor_copy(out=c3.t[:, :, 32:48], in_=v3(pb.t[:, :])), reads=[pb], writes=[c3])
        cst = [self.sb('cst', [48, 512], BF16) for _ in range(2)]
        csem = [self.dsem() for _ in range(2)]
        for b in range(NB):
            pt = self.ps[2 + b % 2]
            for i in range(4):
                self.op('pe', lambda e: e.transpose(out=pt.t[0:48, i * 128:(i + 1) * 128], in_=c3.t[:, 4 * b + i, :], identity=self.ident.t[:]),
                        reads=[c3, self.ident], writes=[pt])
            s_ = cst[b % 2]
            self.op('act', lambda e: e.activation(out=s_.t[:], in_=pt.t[0:48, :], func=AF.Copy), reads=[pt], writes=[s_])
            self.dma('sp', csem[b % 2], [(C3d[:, b * 512:(b + 1) * 512], s_.t[:])], reads=[s_])

    def bcast_row64(self, rb, row, n=512):
        self.mm(rb, rb.t[0:65, 0:n], self.sel64.t[:, 0:65], row.t[0:128, 0:n], True, True, [self.sel64, row])

    def fox_attn(self):
        QT, KT, V, OT, C3 = [self.scr[n][0] for n in ['QT', 'KT', 'V', 'OT', 'C3']]
        Vv = V.rearrange("(kc p) f -> p kc f", p=128)
        C3v = C3.rearrange("(a h) s -> h a s", h=16)
        with self.phase():
            QA = [self.sb('QA', [67, S], BF16) for _ in range(2)]
            KA = [self.sb('KA', [67, S], BF16) for _ in range(2)]
            VA = [self.sb('VA', [128, NT, 65], BF16) for _ in range(2)]
            lsem = [[self.dsem() for _ in range(3)] for _ in range(2)]
            for x in range(2):
                self.op('pool', lambda e: e.memset(KA[x].t[64:67, :], 1.0), writes=[KA[x]])
                self.op('pool', lambda e: e.memset(VA[x].t[:, :, 64:65], 1.0), writes=[VA[x]])
            NSB, NPT, LA = 4, 6, 3
            pts = [self.sb('pt', [128, 512], BF16) for _ in range(NPT)]
            oa = [self.sb('oa', [65, 512], F32) for _ in range(2)]
            lr = [self.sb('lr', [128, 512], F32) for _ in range(2)]
            for b_ in lr:
                self.op('pool', lambda e: e.memset(b_.t[:], 0.0), writes=[b_])
            on = [self.sb('on', [64, 512], BF16) for _ in range(2)]
            osem = [self.dsem('pool') for _ in range(2)]
            lsc = [self.sb('lsc', [65, 512], F32) for _ in range(2)]
            jobs = [(h, qb, kc) for h in range(16) for qb in range(NB) for kc in range(4 * qb + 4)]

            def stage_a(i):
                h, qb, kc = jobs[i]
                x = h % 2
                if qb == 0 and kc == 0:
                    self.dma('sp', lsem[x][0], [(QA[x].t[0:64, :], QT[h * 64:(h + 1) * 64, :]), (QA[x].t[64:67, :], C3v[h])], writes=[QA[x]])
                    self.dma('sp', lsem[x][1], [(KA[x].t[0:64, :], KT[h * 64:(h + 1) * 64, :])], writes=[KA[x]])
                    self.dma('sp', lsem[x][2], [(VA[x].t[:, :, 0:64], Vv[:, :, h * 64:(h + 1) * 64])], writes=[VA[x]])
                j = kc - 4 * qb
                q0 = max(j, 0) * 128
                ps_ = self.ps[i % NSB]
                pt_ = pts[i % NPT]
                self.mm(ps_, ps_.t[:, q0:512], KA[x].t[0:67, kc * 128:(kc + 1) * 128], QA[x].t[0:67, qb * 512 + q0:(qb + 1) * 512],
                        True, True, [KA[x], QA[x]])
                if j >= 0:
                    self.op('dve', lambda e: e.tensor_tensor(out=ps_.t[:, q0:q0 + 128], in0=ps_.t[:, q0:q0 + 128], in1=self.mk.t[:, 0:128], op=ALU.add),
                            reads=[self.mk], writes=[ps_])
                self.op('act', lambda e: e.activation(out=pt_.t[:, q0:512], in_=ps_.t[:, q0:512], func=AF.Exp,
                                                      bias=self.cneg.t[:, kc * 16 + h:kc * 16 + h + 1]), reads=[ps_, self.cneg], writes=[pt_])

            def stage_b(i):
                h, qb, kc = jobs[i]
                x = h % 2
                nf = h * NB + qb
                po = self.ps[4 + nf % 2]
                nk = 4 * qb + 4
                q0 = max(kc - 4 * qb, 0) * 128
                pt_ = pts[i % NPT]
                self.mm(po, po.t[0:65, q0:512], VA[x].t[:, kc, 0:65], pt_.t[:, q0:512], kc == 0, kc == nk - 1, [VA[x], pt_])
                if kc == nk - 1:
                    self.flush(keep=11)
                    self.fox_fin(po, oa[nf % 2], lr[nf % 2], on[nf % 2], osem[nf % 2], self.ps[6 + nf % 2], OT, h, qb, lsc[nf % 2])
                self.pump(1)

            for i in range(len(jobs) + LA):
                if i < len(jobs):
                    stage_a(i)
                if i - LA >= 0:
                    stage_b(i - LA)


    def fox_fin(self, po, oa_, lr_, on_, osem_, rb, OT, h, qb, sc_):
        self.op('dve', lambda e: e.tensor_copy(out=oa_.t[:, :], in_=po.t[0:65, :]), reads=[po], writes=[oa_])
        self.defer([
            lambda: self.op('dve', lambda e: e.reciprocal(out=lr_.t[64:65, :], in_=oa_.t[64:65, :]), reads=[oa_], writes=[lr_]),
            None, None, None, None, None, None,
            lambda: self.bcast_row64(rb, lr_),
            None,
            lambda: self.op('dve', lambda e: e.tensor_tensor(out=on_.t[:, :], in0=oa_.t[0:64, :], in1=rb.t[0:64, :], op=ALU.mult), reads=[oa_, rb], writes=[on_]),
            lambda: self.dma('pool', osem_, [(OT[h * 64:(h + 1) * 64, qb * 512:(qb + 1) * 512], on_.t[:, :])], reads=[on_]),
        ])

    def oproj_pass(self, w_out_d, nk, hsrc, hdst):
        if 'oproj' in SKIP:
            return
        OT = self.scr['OT'][0]
        OTv = OT.rearrange("(c p) s -> p c s", p=128)
        with self.phase():
            Wo = self.sb('wo', [128, nk, D], BF16)
            self.load_w(Wo, w_out_d, nk)
            ot = [self.sb('ot', [128, nk, 512], BF16) for _ in range(2)]
            otsem = [self.dsem() for _ in range(2)]
            hres = [self.sb('hres', [128, D], F32) for _ in range(4)]
            hrsem = [self.dsem() for _ in range(4)]
            hssem = [self.dsem('pool') for _ in range(4)]
            for b in range(NB):
                o_ = ot[b % 2]
                self.dma('sp', otsem[b % 2], [(o_.t[:, :, :], OTv[:, 0:nk, b * 512:(b + 1) * 512])], writes=[o_])
                for i in range(4):
                    ti = 4 * b + i
                    hr = hres[i % 4]
                    self.dma('sp', hrsem[i % 4], [(hr.t[:], hsrc[0][ti * 128:(ti + 1) * 128, :])], reads=[hsrc[1][ti]], writes=[hr])
                    for half in range(2):
                        hsl = slice(half * 512, (half + 1) * 512)
                        po = self.ps[(2 * i + half) % 4]
                        for c in range(nk):
                            self.mm(po, po.t[:, :], o_.t[:, c, i * 128:(i + 1) * 128], Wo.t[:, c, hsl], c == 0, c == nk - 1, [o_, Wo])
                        self.op('dve', lambda e: e.tensor_tensor(out=hr.t[:, hsl], in0=po.t[:], in1=hr.t[:, hsl], op=ALU.add), reads=[po, hr], writes=[hr])
                    self.dma('pool', hssem[i % 4], [(hdst[0][ti * 128:(ti + 1) * 128, :], hr.t[:])], reads=[hr], writes=[hdst[1][ti]])

    def fox_mixer(self, L, gcol, hsrc, hdst):
        I = self.inp
        QT, KT, V = self.scr['QT'][0], self.scr['KT'][0], self.scr['V'][0]
        with ExitStack() as es:
            prev, self.cur = self.cur, es
            self.cneg = self.sb('cneg', [128, 512], F32)
            self.cur = prev
            groups = [('T', 0, QT, 0, 0.125, False), ('T', 512, QT, 512, 0.125, False),
                      ('T', 1024, KT, 0, 1.0, False), ('T', 1536, KT, 512, 1.0, False),
                      ('N', 2048, V, 0, 1.0, False), ('N', 2560, V, 512, 1.0, False)]
            self.proj_pass(I['fox_w_in'], 3088, gcol, hsrc, groups, fox=True)
            self.fox_attn()
            self.barrier()
        self.oproj_pass(I['fox_w_out'], 8, hsrc, hdst)


    def diff_mixer(self, L, gcol, hsrc, hdst):
        I = self.inp
        QT, KT, V, OT = [self.scr[n][0] for n in ['QT', 'KT', 'V', 'OT']]
        groups = [('T', 0, QT, 0, 0.125, True), ('T', 512, QT, 512, 0.125, True),
                  ('T', 1024, KT, 0, 1.0, True), ('T', 1536, KT, 512, 1.0, True),
                  ('N', 2048, V, 0, 1.0, False), ('N', 2560, V, 512, 1.0, False)]
        self.proj_pass(I['diff_w_in'], 3072, gcol, hsrc, groups)
        lam_init = 0.8 - 0.6 * math.exp(-0.3 * L)
        Vv = V.rearrange("(kc p) f -> p kc f", p=128)
        with self.phase():
            lp = self.sb('lp', [65, 256], F32)
            lt = self.sb('lt', [65, 128], F32)
            l2 = self.sb('l2', [65, 4], F32)
            self.dma('sp', self.dsem(), [(lp.t[64:65, :], I['diff_lambda'][0:1, :])], writes=[lp])
            self.op('dve', lambda e: e.tensor_tensor(out=lt.t[64:65, 0:64], in0=lp.t[64:65, 0:64], in1=lp.t[64:65, 64:128], op=ALU.mult), reads=[lp], writes=[lt])
            self.op('dve', lambda e: e.tensor_tensor(out=lt.t[64:65, 64:128], in0=lp.t[64:65, 128:192], in1=lp.t[64:65, 192:256], op=ALU.mult), reads=[lp], writes=[lt])
            self.op('dve', lambda e: e.reduce_sum(out=l2.t[64:65, 0:1], in_=lt.t[64:65, 0:64], axis=AX.X), reads=[lt], writes=[l2])
            self.op('dve', lambda e: e.reduce_sum(out=l2.t[64:65, 1:2], in_=lt.t[64:65, 64:128], axis=AX.X), reads=[lt], writes=[l2])
            self.op('act', lambda e: e.activation(out=l2.t[64:65, 0:2], in_=l2.t[64:65, 0:2], func=AF.Exp), reads=[l2], writes=[l2])
            self.op('dve', lambda e: e.tensor_tensor(out=l2.t[64:65, 2:3], in0=l2.t[64:65, 1:2], in1=l2.t[64:65, 0:1], op=ALU.subtract), reads=[l2], writes=[l2])
            self.op('dve', lambda e: e.tensor_scalar(out=l2.t[64:65, 3:4], in0=l2.t[64:65, 2:3], scalar1=-lam_init, scalar2=None, op0=ALU.add), reads=[l2], writes=[l2])
            gs = self.sb('gs', [64, 2], F32)
            self.dma('sp', self.dsem(), [(gs.t[:, 0:1], I['diff_subln'][0:64, :]), (gs.t[:, 1:2], I['diff_subln'][64:128, :])], writes=[gs])
            self.op('dve', lambda e: e.tensor_scalar(out=gs.t[:], in0=gs.t[:], scalar1=1.0 - lam_init, scalar2=None, op0=ALU.mult), reads=[gs], writes=[gs])
            QA = [self.sb('QA', [128, S], BF16) for _ in range(2)]
            KA = [self.sb('KA', [128, S], BF16) for _ in range(2)]
            VA = [self.sb('VA', [128, NT, 130], BF16) for _ in range(2)]
            lsem = [[self.dsem() for _ in range(2)] for _ in range(2)]
            vsem = [self.dsem() for _ in range(2)]
            for x in range(2):
                self.op('pool', lambda e: e.memset(VA[x].t[:, :, 64:65], 1.0), writes=[VA[x]])
                self.op('pool', lambda e: e.memset(VA[x].t[:, :, 129:130], 0.0), writes=[VA[x]])
                self.op('pool', lambda e: e.memset(QA[x].t[64:128, :], 0.0), writes=[QA[x]])
                self.op('pool', lambda e: e.memset(KA[x].t[64:128, :], 0.0), writes=[KA[x]])
            NSB, NPT, LA = 3, 5, 2
            pts = [self.sb('pt', [128, 512], BF16) for _ in range(NPT)]
            oa0 = self.sb('oa0', [65, S], F32)
            ob0 = self.sb('ob0', [64, S], F32)
            oa1 = [self.sb('oa1', [65, 512], F32) for _ in range(2)]
            ob1 = [self.sb('ob1', [64, 512], F32) for _ in range(2)]
            lr = [self.sb('lr', [128, 1024], F32) for _ in range(2)]
            ta = [self.sb('ta', [64, 512], F32) for _ in range(2)]
            tb = [self.sb('tb', [64, 512], F32) for _ in range(2)]
            sq = [self.sb('sq', [128, 1024], F32) for _ in range(2)]
            for b_ in lr + sq:
                self.op('pool', lambda e: e.memset(b_.t[:], 0.0), writes=[b_])
            msb = [self.sb('msb', [64, 512], F32) for _ in range(2)]
            rsb = [self.sb('rsb', [64, 512], F32) for _ in range(2)]
            on = [self.sb('on', [64, 1024], BF16) for _ in range(2)]
            osem = [self.dsem('pool') for _ in range(2)]
            eps5 = self.sb('eps5', [64, 1], F32)
            self.op('pool', lambda e: e.memset(eps5.t[:], 1e-5), writes=[eps5])
            rsc = [self.sb('rsc', [65, 1024], F32) for _ in range(2)]
            jobs = [(h, m, qb, kc) for h in range(8) for m in range(2) for qb in range(NB) for kc in range(4 * qb + 4)]

            def stage_a(i):
                h, m, qb, kc = jobs[i]
                xv = h % 2
                n = h * 2 + m
                x = n % 2
                if qb == 0 and kc == 0:
                    if m == 0:
                        self.dma('sp', vsem[xv], [(VA[xv].t[:, :, 0:64], Vv[:, :, h * 128:h * 128 + 64]),
                                                  (VA[xv].t[:, :, 65:129], Vv[:, :, h * 128 + 64:h * 128 + 128])], writes=[VA[xv]])
                    self.dma('sp', lsem[x][0], [(QA[x].t[0:64, :], QT[n * 64:(n + 1) * 64, :])], writes=[QA[x]])
                    self.dma('sp', lsem[x][1], [(KA[x].t[0:64, :], KT[n * 64:(n + 1) * 64, :])], writes=[KA[x]])
                j = kc - 4 * qb
                q0 = max(j, 0) * 128
                ps_ = self.ps[i % NSB]
                pt_ = pts[i % NPT]
                self.mm(ps_, ps_.t[:, q0:512], KA[x].t[0:128, kc * 128:(kc + 1) * 128], QA[x].t[0:128, qb * 512 + q0:(qb + 1) * 512],
                        True, True, [KA[x], QA[x]])
                if j >= 0:
                    self.op('dve', lambda e: e.tensor_tensor(out=ps_.t[:, q0:q0 + 128], in0=ps_.t[:, q0:q0 + 128], in1=self.mk.t[:, 0:128], op=ALU.add),
                            reads=[self.mk], writes=[ps_])
                self.op('act', lambda e: e.activation(out=pt_.t[:, q0:512], in_=ps_.t[:, q0:512], func=AF.Exp), reads=[ps_], writes=[pt_])

            def stage_b(i):
                h, m, qb, kc = jobs[i]
                xv = h % 2
                nq = (h * 2 + m) * NB + qb
                pa, pb = self.ps[3 + nq % 2], self.ps[5 + nq % 2]
                nk = 4 * qb + 4
                q0 = max(kc - 4 * qb, 0) * 128
                pt_ = pts[i % NPT]
                self.mm(pa, pa.t[0:65, q0:512], VA[xv].t[:, kc, 0:65], pt_.t[:, q0:512], kc == 0, kc == nk - 1, [VA[xv], pt_])
                self.mm(pb, pb.t[0:65, q0:512], VA[xv].t[:, kc, 65:130], pt_.t[:, q0:512], kc == 0, kc == nk - 1, [VA[xv], pt_])
                if kc == nk - 1:
                    qsl = slice(qb * 512, (qb + 1) * 512)
                    if m == 0:
                        self.op('dve', lambda e: e.tensor_copy(out=oa0.t[:, qsl], in_=pa.t[0:65, :]), reads=[pa], writes=[oa0])
                        self.op('dve', lambda e: e.tensor_copy(out=ob0.t[:, qsl], in_=pb.t[0:64, :]), reads=[pb], writes=[ob0])
                    else:
                        f = (h * NB + qb) % 2
                        self.flush(keep=44)
                        self.diff_fin(pa, pb, oa0, ob0, oa1[f], ob1[f], lr[f], ta[f], tb[f], sq[f], msb[f], rsb[f], on[f], osem[f], l2, gs, eps5, OT, h, qsl, rsc[f])
                self.pump(2)

            for i in range(len(jobs) + LA):
                if i < len(jobs):
                    stage_a(i)
                if i - LA >= 0:
                    stage_b(i - LA)

        self.oproj_pass(I['diff_w_out'], 8, hsrc, hdst)

    def diff_fin(self, pa, pb, oa0, ob0, a1, b1, lr_, ta_, tb_, sq_, ms_, rs_, on_, osem_, l2, gs, eps5, OT, h, qsl, sc_):
        op = self.op
        op('dve', lambda e: e.tensor_copy(out=a1.t[:, :], in_=pa.t[0:65, :]), reads=[pa], writes=[a1])
        op('dve', lambda e: e.tensor_copy(out=b1.t[:, :], in_=pb.t[0:64, :]), reads=[pb], writes=[b1])
        rr = self.ps[7]
        self.defer([
            lambda: op('dve', lambda e: e.reciprocal(out=lr_.t[64:65, 0:512], in_=oa0.t[64:65, qsl]), reads=[oa0], writes=[lr_]),
            None, None,
            lambda: op('dve', lambda e: e.reciprocal(out=lr_.t[64:65, 512:1024], in_=a1.t[64:65, :]), reads=[a1], writes=[lr_]),
            None, None,
            lambda: op('dve', lambda e: e.tensor_scalar(out=lr_.t[64:65, 512:1024], in0=lr_.t[64:65, 512:1024], scalar1=l2.t[64:65, 3:4], scalar2=None, op0=ALU.mult),
                       reads=[lr_, l2], writes=[lr_]),
            None, None, None, None,
            lambda: self.mm(rr, rr.t[0:65, :], self.sel64.t[:, 0:65], lr_.t[0:128, 0:512], True, True, [self.sel64, lr_]),
            None, None,
            lambda: op('dve', lambda e: e.tensor_tensor(out=ta_.t[:, :], in0=oa0.t[0:64, qsl], in1=rr.t[0:64, :], op=ALU.mult), reads=[oa0, rr], writes=[ta_]),
            lambda: op('dve', lambda e: e.tensor_tensor(out=tb_.t[:, :], in0=ob0.t[0:64, qsl], in1=rr.t[0:64, :], op=ALU.mult), reads=[ob0, rr], writes=[tb_]),
            None, None, None,
            lambda: self.mm(rr, rr.t[0:65, :], self.sel64.t[:, 0:65], lr_.t[0:128, 512:1024], True, True, [self.sel64, lr_]),
            None, None,
            lambda: op('dve', lambda e: e.tensor_tensor(out=a1.t[0:64, :], in0=a1.t[0:64, :], in1=rr.t[0:64, :], op=ALU.mult), reads=[rr], writes=[a1]),
            lambda: op('dve', lambda e: e.tensor_tensor(out=b1.t[0:64, :], in0=b1.t[0:64, :], in1=rr.t[0:64, :], op=ALU.mult), reads=[rr], writes=[b1]),
            lambda: op('pool', lambda e: e.tensor_tensor(out=ta_.t[:, :], in0=ta_.t[:, :], in1=a1.t[0:64, :], op=ALU.add), reads=[a1], writes=[ta_]),
            lambda: op('pool', lambda e: e.tensor_tensor(out=tb_.t[:, :], in0=tb_.t[:, :], in1=b1.t[0:64, :], op=ALU.add), reads=[b1], writes=[tb_]),
            lambda: op('act', lambda e: e.activation(out=sq_.t[0:64, 0:512], in_=ta_.t[:, :], func=AF.Square), reads=[ta_], writes=[sq_]),
            lambda: op('act', lambda e: e.activation(out=sq_.t[0:64, 512:1024], in_=tb_.t[:, :], func=AF.Square), reads=[tb_], writes=[sq_]),
            None, None, None,
            lambda: self.mm(rr, rr.t[0:65, :], self.onesT.t[:, 0:65], sq_.t[0:128, 0:512], True, False, [self.onesT, sq_]),
            lambda: self.mm(rr, rr.t[0:65, :], self.onesT.t[:, 0:65], sq_.t[0:128, 512:1024], False, True, [self.onesT, sq_]),
            lambda: op('act', lambda e: e.activation(out=ms_.t[:, :], in_=rr.t[0:64, :], func=AF.Sqrt, scale=1.0 / 128, bias=eps5.t[:, 0:1]), reads=[rr, eps5], writes=[ms_]),
            None, None,
            lambda: op('dve', lambda e: e.reciprocal(out=rs_.t[:, :], in_=ms_.t[:, :]), reads=[ms_], writes=[rs_]),
            None, None, None, None,
            lambda: op('dve', lambda e: e.scalar_tensor_tensor(out=on_.t[:, 0:512], in0=ta_.t[:, :], scalar=gs.t[:, 0:1], in1=rs_.t[:, :], op0=ALU.mult, op1=ALU.mult),
                       reads=[ta_, gs, rs_], writes=[on_]),
            lambda: op('dve', lambda e: e.scalar_tensor_tensor(out=on_.t[:, 512:1024], in0=tb_.t[:, :], scalar=gs.t[:, 1:2], in1=rs_.t[:, :], op0=ALU.mult, op1=ALU.mult),
                       reads=[tb_, gs, rs_], writes=[on_]),
            lambda: self.dma('pool', osem_, [(OT[h * 128:h * 128 + 64, qsl], on_.t[:, 0:512]), (OT[h * 128 + 64:h * 128 + 128, qsl], on_.t[:, 512:1024])], reads=[on_]),
        ])

    def dil_mixer(self, L, gcol, hsrc, hdst):
        I = self.inp
        QT, KT, V, OT = [self.scr[n][0] for n in ['QT', 'KT', 'V', 'OT']]
        groups = []
        for g in range(3):
            for hf in range(2):
                groups.append(('T', g * 1024 + hf * 512, QT, g * 1024 + hf * 512, 0.125, True))
        groups += [('T', 3072, KT, 0, 1.0, True), ('T', 3584, KT, 512, 1.0, True),
                   ('N', 4096, V, 0, 1.0, False), ('N', 4608, V, 512, 1.0, False)]
        self.proj_pass(I['dil_w_in'], 5120, gcol, hsrc, groups)
        dils = [1, 4, 16]
        with self.phase():
            Q3 = [self.sb('Q3', [128, 3, S], BF16) for _ in range(2)]
            KA = [self.sb('KA', [128, S], BF16) for _ in range(2)]
            VD = [[self.sb('VD', [128, NT, 65], BF16) for _ in range(3)] for _ in range(2)]
            lsem = [[self.dsem() for _ in range(5)] for _ in range(2)]
            for x in range(2):
                for g in range(3):
                    self.op('pool', lambda e: e.memset(VD[x][g].t[:, :, 64:65], 1.0), writes=[VD[x][g]])
            acc = [self.sb('acc', [65, S], F32) for _ in range(2)]
            pts = [self.sb('pt', [128, 512], BF16) for _ in range(4)]
            lr = [self.sb('lr', [128, 512], F32) for _ in range(2)]
            for b_ in lr:
                self.op('pool', lambda e: e.memset(b_.t[:], 0.0), writes=[b_])
            lsc = [self.sb('lsc', [65, 512], F32) for _ in range(2)]
            mkb = self.sb('mkb', [128, 512], BF16)
            for c_ in range(4):
                msrc = self.mk.t[:, 768:896] if c_ % 2 == 0 else self.mk.t[:, 640:768]
                self.op('dve', lambda e: e.tensor_copy(out=mkb.t[:, c_ * 128:(c_ + 1) * 128], in_=msrc), reads=[self.mk], writes=[mkb])
            for x in range(2):
                self.op('pool', lambda e: e.memset(Q3[x].t[64:128, :, :], 0.0), writes=[Q3[x]])
                self.op('pool', lambda e: e.memset(KA[x].t[64:128, :], 0.0), writes=[KA[x]])
            on = [self.sb('on', [64, 512], BF16) for _ in range(2)]
            osem = [self.dsem('pool') for _ in range(2)]
            LA = 2
            nf = 0
            for h in range(16):
                x = h % 2
                self.dma('sp', lsem[x][0], [(Q3[x].t[0:64, g, :], QT[g * 1024 + h * 64:g * 1024 + (h + 1) * 64, :]) for g in range(3)], writes=[Q3[x]])
                self.dma('sp', lsem[x][1], [(KA[x].t[0:64, :], KT[h * 64:(h + 1) * 64, :])], writes=[KA[x]])
                for g, d in enumerate(dils):
                    nb = NT // d
                    pairs = []
                    for r in range(d):
                        vsrc = V[r:S:d, h * 64:(h + 1) * 64].rearrange("(n p) f -> p n f", p=128)
                        pairs.append((VD[x][g].t[:, r * nb:(r + 1) * nb, 0:64], vsrc))
                    self.dma('sp', lsem[x][2 + g], pairs, writes=[VD[x][g]])
                jobs = [(g, d, r, n) for g, d in enumerate(dils) for r in range(d) for n in range(0, NT // d, 2)]
                ac = acc[x]

                def tok(d, r, n0, nblk):
                    return slice(r + d * 128 * n0, r + d * 128 * (n0 + nblk - 1) + d * 127 + 1, d)

                for idx in range(len(jobs) + LA):
                    if idx < len(jobs):
                        g, d, r, n = jobs[idx]
                        ps_ = self.ps[idx % 3]
                        pt_ = pts[idx % 4]
                        for s_ in range(2):
                            nn = n + s_
                            qsl = tok(d, r, nn, 1)
                            if nn > 0:
                                self.mm(ps_, ps_.t[:, s_ * 256:s_ * 256 + 128], KA[x].t[0:128, tok(d, r, nn - 1, 1)], Q3[x].t[0:128, g, qsl], True, True, [KA[x], Q3[x]])
                            self.mm(ps_, ps_.t[:, s_ * 256 + 128:s_ * 256 + 256], KA[x].t[0:128, tok(d, r, nn, 1)], Q3[x].t[0:128, g, qsl], True, True, [KA[x], Q3[x]])
                        lo = 128 if n == 0 else 0
                        self.op('act', lambda e: e.activation(out=pt_.t[:, lo:512], in_=ps_.t[:, lo:512], func=AF.Exp), reads=[ps_], writes=[pt_])
                        self.op('pool', lambda e: e.tensor_tensor(out=pt_.t[:, lo:512], in0=pt_.t[:, lo:512], in1=mkb.t[:, lo:512], op=ALU.mult),
                                reads=[mkb], writes=[pt_])
                    jdx = idx - LA
                    if jdx >= 0:
                        g, d, r, n = jobs[jdx]
                        nb = NT // d
                        po = self.ps[3 + jdx % 2]
                        pt_ = pts[jdx % 4]
                        for s_ in range(2):
                            nn = n + s_
                            if nn > 0:
                                self.mm(po, po.t[0:65, s_ * 128:(s_ + 1) * 128], VD[x][g].t[:, r * nb + nn - 1, 0:65], pt_.t[:, s_ * 256:s_ * 256 + 128], True, False, [VD[x][g], pt_])
                            self.mm(po, po.t[0:65, s_ * 128:(s_ + 1) * 128], VD[x][g].t[:, r * nb + nn, 0:65], pt_.t[:, s_ * 256 + 128:s_ * 256 + 256], nn == 0, True, [VD[x][g], pt_])
                        asl = tok(d, r, n, 2)
                        if g == 0:
                            self.op('act', lambda e: e.activation(out=ac.t[:, asl], in_=po.t[0:65, 0:256], func=AF.Copy), reads=[po], writes=[ac])
                        else:
                            self.op('dve', lambda e: e.tensor_tensor(out=ac.t[:, asl], in0=po.t[0:65, 0:256], in1=ac.t[:, asl], op=ALU.add), reads=[po], writes=[ac])
                        self.pump(1)
                self.flush()
                for cb in range(NB):
                    f = nf % 2
                    nf += 1
                    self.dil_fin(ac, lr[f], on[f], osem[f], self.ps[5 + f], OT, h, cb, lsc[f])
        self.oproj_pass(I['dil_w_out'], 8, hsrc, hdst)

    def dil_fin(self, ac, lr_, on_, osem_, rb, OT, h, cb, sc_):
        csl = slice(cb * 512, (cb + 1) * 512)
        self.defer([
            lambda: self.op('dve', lambda e: e.reciprocal(out=lr_.t[64:65, :], in_=ac.t[64:65, csl]), reads=[ac], writes=[lr_]),
            None, None,
            lambda: self.bcast_row64(rb, lr_),
            None,
            lambda: self.op('dve', lambda e: e.tensor_tensor(out=on_.t[:, :], in0=ac.t[0:64, csl], in1=rb.t[0:64, :], op=ALU.mult), reads=[ac, rb], writes=[on_]),
            lambda: self.dma('pool', osem_, [(OT[h * 64:(h + 1) * 64, csl], on_.t[:, :])], reads=[on_]),
        ])

    def sgu_mixer(self, L, gcol, hsrc, hdst):
        I = self.inp
        with self.phase():
            W = self.sb('sw', [128, 8, 4 * D], BF16)
            Wo = self.sb('swo', [128, 16, D], BF16)
            self.load_w(W, I['sgu_w_in'], 8)
            self.load_w(Wo, I['sgu_w_out'], 16)
            gv = self.sb('gv', [128, 2 * D], F32)
            self.dma('sp', self.dsem(), [(gv.t[:], I['sgu_norm_v'][0:1, :].broadcast_to([128, 2 * D]))], writes=[gv])
            bs = self.sb('bs', [128, 8 * 128], F32)
            self.op('pool', lambda e: e.memset(bs.t[:], 0.0), writes=[bs])
            self.dma('sp', self.dsem(), [(bs.t[0:1, :], I['sgu_b_s'].rearrange("g t -> (g t)").unsqueeze(0))], writes=[bs])
            WmT = self.sb('WmT', [128, 8, 128], BF16)
            st = self.alloc_norm()
            ws = st['hs'][0]
            self.dma('sp', st['hsem'][0], [(ws.t[:, g * 128:(g + 1) * 128], I['sgu_w_s'][g]) for g in range(8)], writes=[ws])
            for g in range(8):
                pt = self.ps[g % 2]
                self.op('pe', lambda e: e.transpose(out=pt.t[:, 0:128], in_=ws.t[:, g * 128:(g + 1) * 128], identity=self.ident.t[:]), reads=[ws, self.ident], writes=[pt])
                self.op('dve', lambda e: e.tensor_tensor(out=WmT.t[:, g, :], in0=pt.t[:, 0:128], in1=self.mk.t[:, 640:768], op=ALU.mult), reads=[pt, self.mk], writes=[WmT])
            uT = [self.sb('uT', [128, 4, 512], BF16) for _ in range(4)]
            vt = self.sb('vt', [128, 2 * D], F32)
            jk2 = self.sb('jk2', [128, 2 * D], BF16)
            vn = [self.sb('vn', [128, 2 * D], BF16) for _ in range(2)]
            zT = [self.sb('zT', [128, 16, 128], BF16) for _ in range(2)]
            ss2 = [self.sb('ss2', [128, 1], F32) for _ in range(2)]
            ms2 = [self.sb('ms2', [128, 1], F32) for _ in range(2)]
            rs2 = [self.sb('rs2', [128, 1], F32) for _ in range(2)]
            hres = [self.sb('hres', [128, D], F32) for _ in range(2)]
            hrsem = [self.dsem() for _ in range(2)]
            hssem = [self.dsem() for _ in range(2)]
            self.norm_prep(st, 0, hsrc)
            hnT = self.norm_trans(st, gcol)
            for b in range(NB):
                if b > 0:
                    self.norm_trans(st, gcol)
                if b + 1 < NB:
                    self.norm_prep(st, b + 1, hsrc)
                for fc in range(16):
                    pu = self.ps[fc % 2]
                    for k in range(8):
                        self.mm(pu, pu.t[:, :], W.t[:, k, fc * 128:(fc + 1) * 128], hnT[k].t[:, :], k == 0, k == 7, [W, hnT[k]])
                    u_ = uT[fc // 4]
                    self.op('act', lambda e: e.activation(out=u_.t[:, fc % 4, :], in_=pu.t[:, :], func=AF.Gelu), reads=[pu], writes=[u_])
                for i in range(4):
                    ti = 4 * b + i
                    f = i % 2
                    hr = hres[f]
                    self.dma('sp', hrsem[f], [(hr.t[:], hsrc[0][ti * 128:(ti + 1) * 128, :])], reads=[hsrc[1][ti]], writes=[hr])
                    for vg in range(4):
                        pv = self.ps[2 + vg % 2]
                        for k in range(8):
                            self.mm(pv, pv.t[:, :], hnT[k].t[:, i * 128:(i + 1) * 128], W.t[:, k, 2 * D + vg * 512:2 * D + (vg + 1) * 512], k == 0, k == 7, [hnT[k], W])
                        self.op('act', lambda e: e.activation(out=vt.t[:, vg * 512:(vg + 1) * 512], in_=pv.t[:, :], func=AF.Gelu), reads=[pv], writes=[vt])
                    self.op('act', lambda e: e.activation(out=jk2.t[:], in_=vt.t[:], func=AF.Square, accum_out=ss2[f].t[:, 0:1]), reads=[vt], writes=[jk2, ss2[f]])
                    self.rstd_of(ss2[f], 2 * D, 1e-6, ms2[f], rs2[f])
                    self.op('dve', lambda e: e.scalar_tensor_tensor(out=vn[f].t[:], in0=vt.t[:], scalar=rs2[f].t[:, 0:1], in1=gv.t[:], op0=ALU.mult, op1=ALU.mult),
                            reads=[vt, rs2[f], gv], writes=[vn[f]])
                    for q4 in range(4):
                        pm = self.ps[4 + q4 % 2]
                        for c in range(4):
                            fc = q4 * 4 + c
                            g = fc // 2
                            self.mm(pm, pm.t[:, c * 128:(c + 1) * 128], vn[f].t[:, fc * 128:(fc + 1) * 128], WmT.t[:, g, :], True, False, [vn[f], WmT])
                            self.mm(pm, pm.t[:, c * 128:(c + 1) * 128], self.ones0.t[:, 0:128], bs.t[:, g * 128:(g + 1) * 128], False, True, [self.ones0, bs])
                        self.op('dve', lambda e: e.tensor_tensor(out=zT[f].t[:, q4 * 4:(q4 + 1) * 4, :], in0=pm.t[:, :].rearrange("p (c t) -> p c t", t=128),
                                                                 in1=uT[q4].t[:, :, i * 128:(i + 1) * 128], op=ALU.mult), reads=[pm, uT[q4]], writes=[zT[f]])
                    for half in range(2):
                        hsl = slice(half * 512, (half + 1) * 512)
                        po = self.ps[6 + half]
                        for fc in range(16):
                            self.mm(po, po.t[:, :], zT[f].t[:, fc, :], Wo.t[:, fc, hsl], fc == 0, fc == 15, [zT[f], Wo])
                        self.op('dve', lambda e: e.tensor_tensor(out=hr.t[:, hsl], in0=po.t[:], in1=hr.t[:, hsl], op=ALU.add), reads=[po, hr], writes=[hr])
                    self.dma('sp', hssem[f], [(hdst[0][ti * 128:(ti + 1) * 128, :], hr.t[:])], reads=[hr], writes=[hdst[1][ti]])


INPUT_SHAPES = {
    'x': [S, D], 'p': [4, S, 256],
    'w_ffn1_in': [4, D, 2 * DFF], 'w_ffn1_out': [4, DFF, D],
    'w_ffn2_in': [4, D, 2 * DFF], 'w_ffn2_out': [4, DFF, D],
    'w_ple_gate': [4, D, D], 'b_ple_gate': [4, D], 'w_ple_proj': [4, 256, D],
    'fox_w_in': [D, 3 * D + 16], 'fox_b_f': [1, 16], 'fox_w_out': [D, D],
    'dil_w_in': [D, 5 * D], 'dil_w_out': [D, D],
    'diff_w_in': [D, 3 * D], 'diff_lambda': [1, 256], 'diff_subln': [128, 1], 'diff_w_out': [D, D],
    'sgu_w_in': [D, 4 * D], 'sgu_norm_v': [1, 2 * D], 'sgu_w_s': [8, 128, 128], 'sgu_b_s': [8, 128], 'sgu_w_out': [2 * D, D],
    'norm_final': [1, D],
    'gam': [128, 128], 'ident': [128, 128], 'cs': [128, NT * 16], 'masks': [128, 1024],
}


def build(layers=(0, 1, 2, 3), mixers=True, ffn=True, ple=True, dbg=None):
    nc = bass.Bass("TRN2", target_bir_lowering=False)
    class LazyIn(dict):
        def __missing__(self, n):
            self[n] = nc.dram_tensor(n, INPUT_SHAPES[n], F32, kind="ExternalInput").ap()
            return self[n]
    I = LazyIn()
    out_d = nc.dram_tensor("out", [S, D], F32, kind="ExternalOutput").ap()
    hbuf = nc.dram_tensor("hbuf", [S, D], F32, kind="Internal").ap()
    with ExitStack() as es:
        k = K(nc, es)
        k.inp = I
        k.scr = {}
        for n, shp, dt in [('QT', [3 * D, S], BF16), ('KT', [D, S], BF16), ('V', [S, D], BF16), ('OT', [2 * D, S], BF16),
                           ('C3', [48, S], BF16)]:
            k.scr[n] = (nc.dram_tensor('scr_' + n, shp, dt, kind="Internal").ap(), Buf(None))
        k.outtok = Buf(None)
        xs = (I['x'], [Buf(None) for _ in range(NT)])
        hb = (hbuf, [Buf(None) for _ in range(NT)])
        k.ident = k.sb('ident', [128, 128], F32)
        k.gam = k.sb('gam', [128, 128], F32)
        k.neghalf = k.sb('neghalf', [128, 512], F32)
        k.ones32 = k.sb('ones32', [128, 128], F32)
        k.dma('sp', k.dsem(), [(k.ident.t[:], I['ident'][:, :])], writes=[k.ident])
        k.dma('sp', k.dsem(), [(k.gam.t[:], I['gam'][:, :])], writes=[k.gam])
        k.cs = k.sb('cs', [128, NT * 16], F32)
        k.mk = k.sb('mk', [128, 1024], F32)
        k.dma('sp', k.dsem(), [(k.cs.t[:], I['cs'][:, :])], writes=[k.cs])
        k.dma('sp', k.dsem(), [(k.mk.t[:], I['masks'][:, :])], writes=[k.mk])
        k.op('pool', lambda e: e.memset(k.neghalf.t[:], -0.5), writes=[k.neghalf])
        k.op('pool', lambda e: e.memset(k.ones32.t[:], 1.0), writes=[k.ones32])
        k.sel64 = k.sb('sel64', [128, 65], F32)
        k.onesT = k.sb('onesT', [128, 65], F32)
        k.ones0 = k.sb('ones0', [128, 128], F32)
        for cb_, rows in [(k.sel64, (64, 65)), (k.onesT, (0, 64)), (k.ones0, (0, 1))]:
            k.op('pool', lambda e: e.memset(cb_.t[:], 0.0), writes=[cb_])
            k.op('pool', lambda e: e.memset(cb_.t[rows[0]:rows[1], :], 1.0), writes=[cb_])
        src = xs
        for L in layers:
            if ffn:
                k.ffn_pass(I['w_ffn1_in'][L], I['w_ffn1_out'][L], (0 * 4 + L) * 8, src, hb)
                src = hb
            if mixers:
                getattr(k, ['fox_mixer', 'dil_mixer', 'diff_mixer', 'sgu_mixer'][L])(L, (1 * 4 + L) * 8, src, hb)
                src = hb
            if ffn:
                k.ffn_pass(I['w_ffn2_in'][L], I['w_ffn2_out'][L], (2 * 4 + L) * 8, src, hb)
                src = hb
            if ple:
                last = (L == layers[-1])
                k.ple_pass(L, (3 * 4 + L) * 8, src, hb, final_out=out_d if last else None)
                src = hb
        if not (ple and len(layers)):
            k.final_pass(src, out_d)
        k.barrier()
    nc.used_inputs = list(I.keys())
    return nc


def host_consts():
    c = {}
    c['ident'] = np.eye(128, dtype=np.float32)
    half = 8
    inv_freq = (500000.0 ** (-np.arange(0, 16, 2, dtype=np.float32) / 16)).astype(np.float32)
    ang = np.arange(S, dtype=np.float32)[:, None] * inv_freq[None, :]
    cs = np.concatenate([np.cos(ang), np.sin(ang)], axis=1).astype(np.float32)
    c['cs'] = np.ascontiguousarray(cs.reshape(NT, 128, 16).transpose(1, 0, 2).reshape(128, NT * 16))
    kk = np.arange(128)[:, None]
    qq = np.arange(128)[None, :]
    cur = np.where(kk <= qq, 0.0, NEG).astype(np.float32)
    prev = np.where(kk >= qq, 0.0, NEG).astype(np.float32)
    m = np.zeros((128, 1024), np.float32)
    m[:, 0:128] = cur
    m[:, 128:256] = prev
    m[:, 256:384] = cur
    m[:, 384:512] = prev
    m[:, 512:640] = cur
    m[:, 640:768] = (kk <= qq).astype(np.float32)
    m[:, 768:896] = (kk >= qq).astype(np.float32)
    c['masks'] = m
    return c


def make_in_maps(inputs):
    f = lambda a: np.ascontiguousarray(np.asarray(a, dtype=np.float32))
    shared = {}
    for n in ['w_ffn1_in', 'w_ffn1_out', 'w_ffn2_in', 'w_ffn2_out', 'w_ple_gate', 'b_ple_gate', 'w_ple_proj']:
        shared[n] = f(inputs[n])
    for n in ['fox_w_in', 'fox_b_f', 'fox_w_out', 'dil_w_in', 'dil_w_out', 'diff_w_in', 'diff_w_out', 'sgu_w_in',
              'sgu_norm_v', 'sgu_w_s', 'sgu_b_s', 'sgu_w_out']:
        shared[n] = f(np.asarray(inputs[n])[0])
    shared['fox_b_f'] = shared['fox_b_f'].reshape(1, 16)
    shared['sgu_norm_v'] = shared['sgu_norm_v'].reshape(1, 2 * D)
    shared['diff_lambda'] = f(np.asarray(inputs['diff_lambda'])[0]).reshape(1, 256)
    shared['diff_subln'] = f(np.asarray(inputs['diff_subln'])[0]).reshape(128, 1)
    shared['norm_final'] = f(inputs['norm_final']).reshape(1, D)
    g = np.stack([np.asarray(inputs[n], dtype=np.float32) for n in ['norm_ffn1', 'norm_mix', 'norm_ffn2', 'norm_ple']])
    shared['gam'] = np.ascontiguousarray(g.reshape(16, 8, 128).transpose(2, 0, 1).reshape(128, 128))
    shared.update(host_consts())
    x = np.asarray(inputs['x'], dtype=np.float32)
    p = np.asarray(inputs['p'], dtype=np.float32)
    maps = []
    for b in range(8):
        m = dict(shared)
        m['x'] = np.ascontiguousarray(x[b])
        m['p'] = np.ascontiguousarray(p[:, b])
        maps.append(m)
    return maps


def kernel(**inputs):
    nc = build()
    maps = make_in_maps(inputs)
    maps = [{n: m[n] for n in nc.used_inputs} for m in maps]
    res = run_bass_kernel_spmd(nc, maps, core_ids=list(range(8)))
    return np.stack([np.asarray(r['out'], dtype=np.float32) for r in res.results], axis=0)
```
